# Optimizing a Trainium2 kernel written in Bass

```python
import jax, jax.numpy as jnp
from jax import lax
import numpy as np

D_MODEL = 4096
BATCH = 4
SEQ = 2048
DEPTH = 2
DEC_BATCH = 128
DEC_SEQ = 8
PAST_LEN = 16384
PAGE_SIZE = 128

A_HEADS = 16
A_DK = 128
A_DV = 128
A_QKV = A_HEADS * (2 * A_DK + A_DV)
B_HEADS = 32
B_HEADDIM = 64
B_DINNER = B_HEADS * B_HEADDIM
B_GROUPS = 4
B_REP = B_HEADS // B_GROUPS
B_DSTATE = 128
B_CONV_DIM = B_DINNER + 2 * B_GROUPS * B_DSTATE
C_HEADS = 8
C_DK = 256
C_DV = 256
SHORT_CONV = 4
CHUNK = 64
D_FF = 256 * ((8 * D_MODEL // 3 + 255) // 256)
FFN_CONV = 3
ROPE_BASE = 10000.0
EPS = 1e-6
IN_SPLITS = (A_QKV, A_HEADS * A_DV, A_HEADS, A_HEADS,
             B_DINNER, B_CONV_DIM, B_HEADS,
             C_HEADS * C_DK, C_HEADS * C_DK, C_HEADS * C_DV, C_HEADS * C_DV,
             3 * D_MODEL)
IN_TOTAL = sum(IN_SPLITS)

kernel_name = 'hybrid_gdn_ssd_retention_convffn'


def rmsnorm(x, gain=None):
    xf = x.astype(jnp.float32)
    y = xf * lax.rsqrt(jnp.mean(xf * xf, axis=-1, keepdims=True) + EPS)
    if gain is not None:
        y = y * gain.astype(jnp.float32)
    return y.astype(x.dtype)


def l2norm(x):
    xf = x.astype(jnp.float32)
    return (xf * lax.rsqrt(jnp.sum(xf * xf, axis=-1, keepdims=True) + EPS)).astype(x.dtype)


def causal_dwconv(x, buf, w, b=None):
    width, length = w.shape[0], x.shape[1]
    xc = jnp.concatenate([buf.astype(x.dtype), x], axis=1)
    y = xc[:, 0:length] * w[0]
    for j in range(1, width):
        y = y + xc[:, j:j + length] * w[j]
    if b is not None:
        y = y + b
    return y, xc[:, length:]


def chunk_len(length):
    return CHUNK if length % CHUNK == 0 else length


def to_chunks(t, c):
    bsz, length = t.shape[:2]
    return jnp.swapaxes(t.reshape((bsz, length // c, c) + t.shape[2:]), 0, 1)


def from_chunks(t):
    n, bsz, c = t.shape[:3]
    return jnp.swapaxes(t, 0, 1).reshape((bsz, n * c) + t.shape[3:])


def rotary(t, pos):
    half = t.shape[-1] // 2
    inv = ROPE_BASE ** (-jnp.arange(half, dtype=jnp.float32) / half)
    ang = pos.astype(jnp.float32)[:, None] * inv[None, :]
    cos, sin = jnp.cos(ang)[None, :, None, :], jnp.sin(ang)[None, :, None, :]
    tf = t.astype(jnp.float32)
    t1, t2 = tf[..., :half], tf[..., half:]
    return jnp.concatenate([t1 * cos - t2 * sin, t1 * sin + t2 * cos], axis=-1).astype(t.dtype)


def gated_delta_rule(q, k, v, g, beta, s0):
    f32 = jnp.float32
    c = chunk_len(q.shape[1])
    dv = v.shape[-1]
    tri = jnp.tril(jnp.ones((c, c), bool))
    strict = jnp.tril(jnp.ones((c, c), bool), -1)
    eye = jnp.eye(c, dtype=f32)
    gcum = jnp.cumsum(to_chunks(g.astype(f32), c), axis=2)
    xs = (to_chunks(q.astype(f32), c), to_chunks(k.astype(f32), c), to_chunks(v.astype(f32), c),
          to_chunks(beta.astype(f32), c), gcum)

    def step(s, inp):
        qi, ki, vi, bi, gi = inp
        gt = jnp.swapaxes(gi, 1, 2)
        decay = jnp.exp(jnp.where(tri, gt[..., :, None] - gt[..., None, :], -jnp.inf))
        kb = ki * bi[..., None]
        lmat = jnp.where(strict, jnp.einsum('bihk,bjhk->bhij', kb, ki) * decay, 0.0)
        rhs = jnp.concatenate([vi * bi[..., None], kb * jnp.exp(gi)[..., None]], axis=-1)
        sol = lax.linalg.triangular_solve(lmat + eye, jnp.swapaxes(rhs, 1, 2),
                                          left_side=True, lower=True, unit_diagonal=True)
        u, w = sol[..., :dv], sol[..., dv:]
        v_new = u - jnp.einsum('bhck,bhkv->bhcv', w, s)
        qk = jnp.where(tri, jnp.einsum('bihk,bjhk->bhij', qi, ki) * decay, 0.0)
        o = (jnp.einsum('bihk,bhkv->bihv', qi * jnp.exp(gi)[..., None], s)
             + jnp.einsum('bhij,bhjv->bihv', qk, v_new))
        g_last = gt[..., -1]
        s = (s * jnp.exp(g_last)[..., None, None]
             + jnp.einsum('bihk,bhiv->bhkv', ki * jnp.exp(g_last[:, None, :] - gi)[..., None], v_new))
        return s, o

    s_fin, o = lax.scan(step, s0.astype(f32), xs)
    return from_chunks(o).astype(v.dtype), s_fin.astype(s0.dtype)


def decay_linear_attention(q, k, v, g, s0):
    f32 = jnp.float32
    c = chunk_len(q.shape[1])
    tri = jnp.tril(jnp.ones((c, c), bool))
    gcum = jnp.cumsum(to_chunks(g.astype(f32), c), axis=2)
    xs = (to_chunks(q.astype(f32), c), to_chunks(k.astype(f32), c), to_chunks(v.astype(f32), c), gcum)

    def step(s, inp):
        qi, ki, vi, gi = inp
        gt = jnp.moveaxis(gi, 1, -1)
        decay = jnp.exp(jnp.where(tri, gt[..., :, None] - gt[..., None, :], -jnp.inf))
        scores = jnp.einsum('bigk,bjgk->bgij', qi, ki)[:, :, None] * decay
        o = (jnp.einsum('bgrij,bjgrv->bigrv', scores, vi)
             + jnp.einsum('bigk,bgrkv->bigrv', qi, s) * jnp.exp(gi)[..., None])
        g_last = gi[:, -1]
        s = (s * jnp.exp(g_last)[..., None, None]
             + jnp.einsum('bjgk,bjgrv->bgrkv', ki, vi * jnp.exp(g_last[:, None] - gi)[..., None]))
        return s, o

    s_fin, o = lax.scan(step, s0.astype(f32), xs)
    return from_chunks(o).astype(v.dtype), s_fin.astype(s0.dtype)


def state_shapes():
    return ((A_HEADS, A_DK, A_DV), (SHORT_CONV - 1, A_QKV),
            (B_HEADS, B_DSTATE, B_HEADDIM), (SHORT_CONV - 1, B_CONV_DIM),
            (C_HEADS, C_DK, C_DV), (FFN_CONV - 1, D_FF))


def hybrid_layer(x, st, p, pos0):
    s_gdn, c_gdn, s_ssm, c_ssm, s_ret, c_ffn = st
    bsz, length, _ = x.shape
    u = rmsnorm(x, p['norm_mix'])
    proj = u @ p['w_in']
    offs = np.cumsum(IN_SPLITS)[:-1].tolist()
    (a_qkv, a_z, a_b, a_a, b_z, b_xbc, b_dt, c_q, c_k, c_v, c_g, gates) = jnp.split(proj, offs, axis=-1)

    a_qkv, c_gdn_new = causal_dwconv(a_qkv, c_gdn, p['gdn_conv_w'])
    a_qkv = jax.nn.silu(a_qkv)
    aq, ak, av = jnp.split(a_qkv, [A_HEADS * A_DK, 2 * A_HEADS * A_DK], axis=-1)
    aq = l2norm(aq.reshape(bsz, length, A_HEADS, A_DK)) * (A_DK ** -0.5)
    ak = l2norm(ak.reshape(bsz, length, A_HEADS, A_DK))
    av = av.reshape(bsz, length, A_HEADS, A_DV)
    a_beta = jax.nn.sigmoid(a_b)
    a_g = -jnp.exp(p['gdn_a_log']) * jax.nn.softplus(a_a + p['gdn_dt_bias'])
    ao, s_gdn_new = gated_delta_rule(aq, ak, av, a_g, a_beta, s_gdn)
    ya = (rmsnorm(ao, p['gdn_norm']) * jax.nn.silu(a_z.reshape(bsz, length, A_HEADS, A_DV))).reshape(bsz, length, -1)

    xbc, c_ssm_new = causal_dwconv(b_xbc, c_ssm, p['ssm_conv_w'], p['ssm_conv_b'])
    xbc = jax.nn.silu(xbc)
    bx, bB, bC = jnp.split(xbc, [B_DINNER, B_DINNER + B_GROUPS * B_DSTATE], axis=-1)
    bx = bx.reshape(bsz, length, B_GROUPS, B_REP, B_HEADDIM)
    bB = bB.reshape(bsz, length, B_GROUPS, B_DSTATE)
    bC = bC.reshape(bsz, length, B_GROUPS, B_DSTATE)
    dt = jax.nn.softplus(b_dt + p['ssm_dt_bias']).reshape(bsz, length, B_GROUPS, B_REP)
    a_ssm = -jnp.exp(p['ssm_a_log']).reshape(B_GROUPS, B_REP)
    yb, s_ssm_new = decay_linear_attention(bC, bB, bx * dt[..., None], dt * a_ssm,
                                           s_ssm.reshape(bsz, B_GROUPS, B_REP, B_DSTATE, B_HEADDIM))
    yb = yb + p['ssm_d'].reshape(B_GROUPS, B_REP)[:, :, None] * bx
    yb = (yb.reshape(bsz, length, B_DINNER) * jax.nn.silu(b_z)).reshape(bsz, length, B_GROUPS, B_DINNER // B_GROUPS)
    yb = rmsnorm(yb, p['ssm_norm'].reshape(B_GROUPS, -1)).reshape(bsz, length, B_DINNER)
    s_ssm_new = s_ssm_new.reshape(bsz, B_HEADS, B_DSTATE, B_HEADDIM)

    pos = pos0 + jnp.arange(length)
    cq = rotary(c_q.reshape(bsz, length, C_HEADS, C_DK), pos)
    ck = rotary(c_k.reshape(bsz, length, C_HEADS, C_DK), pos) * (C_DK ** -0.5)
    cv = c_v.reshape(bsz, length, C_HEADS, 1, C_DV)
    log_gamma = jnp.log(1.0 - 2.0 ** (-5.0 - jnp.arange(C_HEADS, dtype=jnp.float32)))
    c_logdecay = jnp.broadcast_to(log_gamma[:, None], (bsz, length, C_HEADS, 1))
    yc, s_ret_new = decay_linear_attention(cq, ck, cv, c_logdecay, s_ret[:, :, None])
    yc = rmsnorm(yc[:, :, :, 0]).reshape(bsz, length, -1) * jax.nn.silu(c_g)
    s_ret_new = s_ret_new[:, :, 0]

    ga, gb, gc = jnp.split(jax.nn.sigmoid(gates), 3, axis=-1)
    h = ga * (ya @ p['w_branch_a']) + gb * (yb @ p['w_branch_b']) + gc * (yc @ p['w_branch_c'])
    x = x + h @ p['w_out']

    u2 = rmsnorm(x, p['norm_ffn'])
    fg, c_ffn_new = causal_dwconv(u2 @ p['w_ffn_gate'], c_ffn, p['ffn_conv_w'], p['ffn_conv_b'])
    x = x + (jax.nn.silu(fg) * (u2 @ p['w_ffn_up'])) @ p['w_ffn_down']
    return x, (s_gdn_new, c_gdn_new, s_ssm_new, c_ssm_new, s_ret_new, c_ffn_new)


def run_trunk(x, states, params, norm_final, pos0):
    new_states = []
    for layer in range(DEPTH):
        p = {name: w[layer] for name, w in params.items()}
        x, ns = hybrid_layer(x, tuple(s[layer] for s in states), p, pos0)
        new_states.append(ns)
    y = rmsnorm(x, norm_final)
    return y, [jnp.stack([ns[i] for ns in new_states]) for i in range(len(states))]


def setup_inputs(seed: int = 0) -> dict:
    key = jax.random.key(seed)
    ks = jax.random.split(key, 32)
    f32 = jnp.float32

    def nrm(i, shape, scale):
        return scale * jax.random.normal(ks[i], shape, f32)

    def dt_bias_init(i, shape):
        dt = jnp.exp(jax.random.uniform(ks[i], shape, f32, float(np.log(1e-3)), float(np.log(1e-1))))
        return dt + jnp.log(-jnp.expm1(-dt))

    def a_log_init(i, shape):
        return jnp.log(jax.random.uniform(ks[i], shape, f32, 1.0, 16.0))

    L = DEPTH
    return {
        'x_prompt': nrm(0, (BATCH, SEQ, D_MODEL), 1.0),
        'x_sample': nrm(1, (DEC_BATCH, DEC_SEQ, D_MODEL), 1.0),
        'state_gdn': nrm(2, (L, DEC_BATCH, A_HEADS, A_DK, A_DV), 0.1),
        'state_gdn_conv': nrm(3, (L, DEC_BATCH, SHORT_CONV - 1, A_QKV), 1.0),
        'state_ssm': nrm(4, (L, DEC_BATCH, B_HEADS, B_DSTATE, B_HEADDIM), 0.1),
        'state_ssm_conv': nrm(5, (L, DEC_BATCH, SHORT_CONV - 1, B_CONV_DIM), 1.0),
        'state_ret': nrm(6, (L, DEC_BATCH, C_HEADS, C_DK, C_DV), 1.0),
        'state_ffn_conv': nrm(7, (L, DEC_BATCH, FFN_CONV - 1, D_FF), 1.0),
        'norm_mix': 1.0 + nrm(8, (L, D_MODEL), 0.02),
        'w_in': nrm(9, (L, D_MODEL, IN_TOTAL), D_MODEL ** -0.5),
        'gdn_conv_w': nrm(10, (L, SHORT_CONV, A_QKV), SHORT_CONV ** -0.5),
        'gdn_a_log': a_log_init(11, (L, A_HEADS)),
        'gdn_dt_bias': dt_bias_init(12, (L, A_HEADS)),
        'gdn_norm': 1.0 + nrm(13, (L, A_DV), 0.02),
        'ssm_conv_w': nrm(14, (L, SHORT_CONV, B_CONV_DIM), SHORT_CONV ** -0.5),
        'ssm_conv_b': nrm(15, (L, B_CONV_DIM), 0.02),
        'ssm_a_log': a_log_init(16, (L, B_HEADS)),
        'ssm_dt_bias': dt_bias_init(17, (L, B_HEADS)),
        'ssm_d': 1.0 + nrm(18, (L, B_HEADS), 0.1),
        'ssm_norm': 1.0 + nrm(19, (L, B_DINNER), 0.02),
        'w_branch_a': nrm(20, (L, A_HEADS * A_DV, D_MODEL), (A_HEADS * A_DV) ** -0.5),
        'w_branch_b': nrm(21, (L, B_DINNER, D_MODEL), B_DINNER ** -0.5),
        'w_branch_c': nrm(22, (L, C_HEADS * C_DV, D_MODEL), (C_HEADS * C_DV) ** -0.5),
        'w_out': nrm(23, (L, D_MODEL, D_MODEL), D_MODEL ** -0.5),
        'norm_ffn': 1.0 + nrm(24, (L, D_MODEL), 0.02),
        'w_ffn_gate': nrm(25, (L, D_MODEL, D_FF), D_MODEL ** -0.5),
        'w_ffn_up': nrm(26, (L, D_MODEL, D_FF), D_MODEL ** -0.5),
        'ffn_conv_w': nrm(27, (L, FFN_CONV, D_FF), FFN_CONV ** -0.5),
        'ffn_conv_b': nrm(28, (L, D_FF), 0.02),
        'w_ffn_down': nrm(29, (L, D_FF, D_MODEL), D_FF ** -0.5),
        'norm_final': 1.0 + nrm(30, (D_MODEL,), 0.02),
    }


def reference(x_prompt, x_sample, state_gdn, state_gdn_conv, state_ssm, state_ssm_conv, state_ret, state_ffn_conv,
              norm_mix, w_in, gdn_conv_w, gdn_a_log, gdn_dt_bias, gdn_norm,
              ssm_conv_w, ssm_conv_b, ssm_a_log, ssm_dt_bias, ssm_d, ssm_norm,
              w_branch_a, w_branch_b, w_branch_c, w_out,
              norm_ffn, w_ffn_gate, w_ffn_up, ffn_conv_w, ffn_conv_b, w_ffn_down, norm_final):
    params = dict(norm_mix=norm_mix, w_in=w_in, gdn_conv_w=gdn_conv_w, gdn_a_log=gdn_a_log,
                  gdn_dt_bias=gdn_dt_bias, gdn_norm=gdn_norm, ssm_conv_w=ssm_conv_w, ssm_conv_b=ssm_conv_b,
                  ssm_a_log=ssm_a_log, ssm_dt_bias=ssm_dt_bias, ssm_d=ssm_d, ssm_norm=ssm_norm,
                  w_branch_a=w_branch_a, w_branch_b=w_branch_b, w_branch_c=w_branch_c, w_out=w_out,
                  norm_ffn=norm_ffn, w_ffn_gate=w_ffn_gate, w_ffn_up=w_ffn_up, ffn_conv_w=ffn_conv_w,
                  ffn_conv_b=ffn_conv_b, w_ffn_down=w_ffn_down)
    prompt_states = tuple(jnp.zeros((DEPTH, BATCH) + shape, x_prompt.dtype) for shape in state_shapes())
    y_prompt, (gdn_p, gdn_conv_p, ssm_p, ssm_conv_p, ret_p, ffn_conv_p) = run_trunk(
        x_prompt, prompt_states, params, norm_final, 0)
    sample_states = (state_gdn, state_gdn_conv, state_ssm, state_ssm_conv, state_ret, state_ffn_conv)
    y_sample, (gdn_s, gdn_conv_s, ssm_s, ssm_conv_s, ret_s, ffn_conv_s) = run_trunk(
        x_sample, sample_states, params, norm_final, PAST_LEN)
    return (y_prompt, y_sample,
            gdn_p, gdn_conv_p, ssm_p, ssm_conv_p, ret_p, ffn_conv_p,
            gdn_s, gdn_conv_s, ssm_s, ssm_conv_s, ret_s, ffn_conv_s)
```

```python
import numpy as np
import concourse.bass as bass
import concourse.mybir as mybir
from concourse.bass_utils import run_bass_kernel_spmd

F32 = mybir.dt.float32
BF16 = mybir.dt.bfloat16
I32 = mybir.dt.int32
ALU = mybir.AluOpType
AF = mybir.ActivationFunctionType
AX = mybir.AxisListType

ENGS = ("pe", "act", "dve", "pool", "sp")
EPOCH = 20000


class Res:
    __slots__ = ("name", "w", "r", "dsem", "dval", "port", "nosame")

    def __init__(self, name):
        self.name = name
        self.port = None
        self.nosame = False
        self.w = []
        self.r = {}
        self.dsem = None
        self.dval = 0


class Prog:
    def __init__(self, nc, strict_same=True):
        self.nc = nc
        self.ops = {e: [] for e in ENGS}
        self.esem = {e: nc.alloc_semaphore(name=f"q_{e}_0") for e in ENGS}
        self.eidx = {e: 0 for e in ENGS}
        self.ecnt = {e: 0 for e in ENGS}
        self.seen = {e: {} for e in ENGS}
        self.strict_same = strict_same
        self.nsem = len(ENGS)
        self.ninstr = 0
        self.all_dsems = {}

    def res(self, name):
        return Res(name)

    def _new_sem(self, name):
        self.nsem += 1
        assert self.nsem < 230, "too many semaphores"
        return self.nc.alloc_semaphore(name=name)

    def _collect(self, e, reads, writes):
        waits = {}
        mysem = self.esem[e]

        def add(dep, kind):
            sem, val = dep
            if sem is mysem or any(sem is s for s in ()):
                if e in ("pe", "sp"):
                    return
                if not self.strict_same or kind == "war":
                    return
            if self.seen[e].get(id(sem), 0) >= val:
                return
            k = id(sem)
            if k not in waits or waits[k][1] < val:
                waits[k] = (sem, val)

        for b in reads:
            for d in b.w:
                add(d, "raw")
        for b in writes:
            kind_w = "war" if b.nosame else "waw"
            for d in b.w:
                add(d, kind_w)
            for k, d in b.r.items():
                add(d, "war")
        out = list(waits.values())
        for sem, val in out:
            self.seen[e][id(sem)] = val
        return out

    def emit(self, e, fn, reads=(), writes=(), signal=True):
        ports = []
        for b in list(reads) + list(writes):
            if b.port is not None and not any(b.port is p for p in ports):
                ports.append(b.port)
        if ports:
            writes = list(writes) + ports
        waits = self._collect(e, reads, writes)
        if signal:
            if self.ecnt[e] >= EPOCH:
                self.eidx[e] += 1
                self.esem[e] = self._new_sem(f"q_{e}_{self.eidx[e]}")
                self.ecnt[e] = 0
            self.ecnt[e] += 1
            dep = (self.esem[e], self.ecnt[e])
            incs = [(self.esem[e], 1)]
        else:
            dep = (self.esem[e], self.ecnt[e] + 1)
            incs = []
        self.ops[e].append((waits, fn, incs))
        self.ninstr += 1
        for b in writes:
            b.w = [dep]
            b.r = {}
        for b in reads:
            b.r[id(dep[0])] = dep
        return dep

    def dma(self, e, out, in_, reads=(), writes=(), sem_res=None, **kw):
        waits = self._collect(e, reads, writes)
        sr = sem_res if sem_res is not None else (writes[0] if writes else reads[0])
        if sr.dsem is None:
            sr.dsem = self._new_sem(f"d_{sr.name}")
            sr.dval = 0
        if sr.dval >= EPOCH * 16:
            sr.dsem = self._new_sem(f"d_{sr.name}_n")
            sr.dval = 0
        sr.dval += 16
        dep = (sr.dsem, sr.dval)
        self.all_dsems[id(sr.dsem)] = dep

        def fn(eng, out=out, in_=in_, kw=kw):
            return eng.dma_start(out=out, in_=in_, **kw)

        self.ops[e].append((waits, fn, [(sr.dsem, 16)]))
        self.ninstr += 1
        for b in writes:
            b.w = [dep] if not (b.w and b.w[0][0] is sr.dsem and False) else b.w
            b.r = {}
        for b in reads:
            b.r[id(dep[0])] = dep
        return dep

    def barrier(self):
        snap = [(self.esem[e], self.ecnt[e]) for e in ENGS if self.ecnt[e] > 0]
        dsn = list(self.all_dsems.values())
        for e in ENGS:
            w2 = []
            for s_, v in snap + dsn:
                if s_ is self.esem[e]:
                    continue
                if self.seen[e].get(id(s_), 0) < v:
                    w2.append((s_, v))
                    self.seen[e][id(s_)] = v
            self.ops[e].append((w2, None, []))

    def dma_multi_write(self, b, deps):
        b.w = list(deps)
        b.r = {}

    def finish(self, final_res=()):
        nc = self.nc
        waits = list(self.all_dsems.values())
        self.ops["sp"].append((waits, None, []))
        with nc.Block() as block:
            def runner(e):
                def _(eng):
                    for waits, fn, incs in self.ops[e]:
                        for sem, val in waits:
                            eng.wait_ge(sem, val)
                        if fn is None:
                            continue
                        ins = fn(eng)
                        for sem, n in incs:
                            ins.then_inc(sem, n)
                return _
            block.tensor(runner("pe"))
            block.scalar(runner("act"))
            block.vector(runner("dve"))
            block.gpsimd(runner("pool"))
            block.sync(runner("sp"))


class Cfg:
    def __init__(self, D=4096, AH=16, BH=32, BG=4, CH=8, DFF=11008, L=2, TP=2048, PB=512, NS=16,
                 past=16384):
        self.D, self.AH, self.BH, self.BG, self.CH, self.DFF, self.L = D, AH, BH, BG, CH, DFF, L
        self.TP, self.PB, self.NS, self.past = TP, PB, NS, past
        self.KC = D // 128
        self.ADK = self.ADV = 128
        self.AQKV = AH * 384
        self.BHD, self.BN = 64, 128
        self.BREP = BH // BG
        self.BDI = BH * 64
        self.BCONV = self.BDI + 2 * BG * 128
        self.CDK = self.CDV = 256
        self.SL = 8
        self.TS = NS * 8
        self.TT = TP + self.TS
        self.FC = DFF // 128
        assert DFF % 128 == 0 and TP % PB == 0 and PB % 64 == 0 and self.TS % 64 == 0
        sp = [self.AQKV, AH * 128, AH, AH, self.BDI, self.BCONV, BH, CH * 256, CH * 256, CH * 256, CH * 256, 3 * D]
        self.splits = sp
        offs = np.cumsum([0] + sp)
        (self.o_aqkv, self.o_az, self.o_ab, self.o_aa, self.o_bz, self.o_bxbc, self.o_bdt,
         self.o_cq, self.o_ck, self.o_cv, self.o_cg, self.o_gates, self.INTOT) = [int(v) for v in offs]
        self.blocks = [(i * PB, PB, 1, PB) for i in range(TP // PB)] + [(TP, self.TS, NS, 8)]


REAL = Cfg(PB=256)


def _mk(fn, *a, **k):
    return lambda eng: fn(eng, *a, **k)


class K:
    def __init__(self, cfg, dbg=()):
        self.c = cfg
        self.nc = bass.Bass("TRN2", target_bir_lowering=False)
        self.P = Prog(self.nc)
        self.dbg = set(dbg)
        self.dbg_out = {}
        self.out_res = self.P.res("outputs")
        self.outs = {}
        self.ins = {}
        self._uid = 0
        self.upto = None
        self.stage = 0
        self.sub = None
        import os
        self.gstop = os.environ.get("GSTOP")

    def din(self, name, shape, dt=F32):
        t = self.nc.dram_tensor(name, list(shape), dt, kind="ExternalInput").ap()
        self.ins[name] = (tuple(shape), dt)
        return t

    def dout(self, name, shape, dt=F32):
        t = self.nc.dram_tensor(name, list(shape), dt, kind="ExternalOutput").ap()
        self.outs[name] = tuple(shape)
        return t

    def dscr(self, name, shape, dt=F32):
        return self.nc.dram_tensor(name, list(shape), dt).ap()

    def sb(self, name, shape, dt=F32):
        return self.nc.alloc_sbuf_tensor("sb_" + name, list(shape), dt)

    def R(self, name):
        return self.P.res(name)

    def dump(self, name, ap, res, shape):
        if name not in self.dbg:
            return
        self._uid += 1
        nm = f"dbg_{name}"
        if nm not in self.dbg_out:
            self.dbg_out[nm] = self.dout(nm, shape, ap.dtype)
        self.P.dma("sp", self.dbg_out[nm], ap, reads=[res], sem_res=self.out_res)

    def mm(self, out, lhsT, rhs, start, stop, reads, writes, signal=True):
        self.P.emit("pe", lambda t: t.matmul(out, lhsT=lhsT, rhs=rhs, start=start, stop=stop),
                    reads, writes, signal=signal)

    def tr(self, out, in_, ident, reads, writes):
        self.P.emit("pe", lambda t: t.transpose(out, in_, ident), reads, writes)

    def act(self, out, in_, func, reads, writes, **kw):
        self.P.emit("act", lambda a: a.activation(out=out, in_=in_, func=func, **kw), reads, writes)

    def cp(self, eng, out, in_, reads, writes):
        if eng == "act":
            self.P.emit("act", lambda a: a.activation(out=out, in_=in_, func=AF.Copy), reads, writes)
        else:
            self.P.emit(eng, lambda v: v.tensor_copy(out=out, in_=in_), reads, writes)

    def tt(self, out, in0, in1, op, reads, writes, eng="dve"):
        self.P.emit(eng, lambda v: v.tensor_tensor(out=out, in0=in0, in1=in1, op=op), reads, writes)

    def ts(self, out, in0, s1, op0, reads, writes, s2=None, op1=None, eng="dve"):
        if op1 is None:
            self.P.emit(eng, lambda v: v.tensor_scalar(out=out, in0=in0, scalar1=s1, scalar2=None, op0=op0), reads, writes)
        else:
            self.P.emit(eng, lambda v: v.tensor_scalar(out=out, in0=in0, scalar1=s1, scalar2=s2, op0=op0, op1=op1), reads, writes)

    def stt(self, out, in0, scalar, in1, op0, op1, reads, writes, eng="dve"):
        self.P.emit(eng, lambda v: v.scalar_tensor_tensor(out=out, in0=in0, scalar=scalar, in1=in1, op0=op0, op1=op1), reads, writes)

    def rsum(self, out, in_, reads, writes):
        self.P.emit("dve", lambda v: v.reduce_sum(out=out, in_=in_, axis=AX.X), reads, writes)

    def memset(self, eng, ap, val, writes):
        self.P.emit(eng, lambda v: v.memset(ap, val), [], writes)

    def ld(self, out, in_, writes, reads=(), q="sp", **kw):
        return self.P.dma(q, out, in_, reads=list(reads), writes=list(writes), **kw)

    def st(self, out, in_, reads, writes=(), q="sp", sem_res=None, **kw):
        return self.P.dma(q, out, in_, reads=list(reads), writes=list(writes), sem_res=sem_res, **kw)

    def setup(self):
        c, nc = self.c, self.nc
        L, KC, NS = c.L, c.KC, c.NS
        d = self.d = {}
        d["xT"] = self.din("xT", [KC, 128, c.TT])
        d["g_mix"] = self.din("g_mix", [L, 128, KC]); d["g_ffn"] = self.din("g_ffn", [L, 128, KC])
        d["g_fin"] = self.din("g_fin", [128, KC])
        d["w_in"] = self.din("w_in", [L, c.D, c.INTOT])
        d["w_ba"] = self.din("w_ba", [L, c.AH * 128, c.D]); d["w_bb"] = self.din("w_bb", [L, c.BDI, c.D])
        d["w_bc"] = self.din("w_bc", [L, c.CH * 256, c.D]); d["w_out"] = self.din("w_out", [L, c.D, c.D])
        d["w_fg"] = self.din("w_fg", [L, c.D, c.DFF]); d["w_fu"] = self.din("w_fu", [L, c.D, c.DFF])
        d["w_fd"] = self.din("w_fd", [L, c.DFF, c.D])
        NA, NB = c.AQKV // 128, c.BCONV // 128
        self.NA, self.NB = NA, NB
        d["a_cw"] = self.din("a_cw", [L, 128, NA, 4]); d["b_cw"] = self.din("b_cw", [L, 128, NB, 4])
        d["b_cb"] = self.din("b_cb", [L, 128, NB]); d["f_cw"] = self.din("f_cw", [L, 128, c.FC, 3])
        d["f_cb"] = self.din("f_cb", [L, 128, c.FC])
        d["a_alog"] = self.din("a_alog", [L, 1, c.AH]); d["a_dtb"] = self.din("a_dtb", [L, 1, c.AH])
        d["a_norm"] = self.din("a_norm", [L, 128, 1])
        d["b_alog"] = self.din("b_alog", [L, 1, c.BH]); d["b_dtb"] = self.din("b_dtb", [L, 1, c.BH])
        d["b_d"] = self.din("b_d", [L, 1, c.BH]); d["b_norm"] = self.din("b_norm", [L, 128, c.BDI // 128])
        d["s_gdn"] = self.din("s_gdn", [L, NS, c.AH, 128, 128])
        d["s_gcv"] = self.din("s_gcv", [L, 128, NA, NS, 3])
        d["s_ssm"] = self.din("s_ssm", [L, NS, c.BH, 128, 64])
        d["s_scv"] = self.din("s_scv", [L, 128, NB, NS, 3])
        d["s_ret"] = self.din("s_ret", [L, NS, c.CH, 256, 256])
        d["s_fcv"] = self.din("s_fcv", [L, 128, c.FC, NS, 2])
        d["c_ident"] = self.din("c_ident", [128, 128]); d["c_ones"] = self.din("c_ones", [128, 128])
        d["c_m64"] = self.din("c_m64", [64, 2, 4, 64])
        d["c_colm"] = self.din("c_colm", [128, 8, 64])
        d["c_rowm"] = self.din("c_rowm", [64, 8])
        d["c_rot"] = self.din("c_rot", [128, 2, c.TT])
        d["c_rdt"] = self.din("c_rdt", [64, 2, c.CH, 64])
        d["c_rqd"] = self.din("c_rqd", [128, 2, c.CH, 64])
        d["c_rkd"] = self.din("c_rkd", [64, 2, c.CH, 8])
        d["yT"] = self.dout("yT", [KC, 128, c.TT])
        d["o_gdn_p"] = self.dout("o_gdn_p", [L, c.AH, 128, 128]); d["o_gcv_p"] = self.dout("o_gcv_p", [L, 128, NA, 3])
        d["o_ssm_p"] = self.dout("o_ssm_p", [L, c.BH, 128, 64]); d["o_scv_p"] = self.dout("o_scv_p", [L, 128, NB, 3])
        d["o_ret_p"] = self.dout("o_ret_p", [L, c.CH, 256, 256]); d["o_fcv_p"] = self.dout("o_fcv_p", [L, 128, c.FC, 2])
        d["o_gdn_s"] = self.dout("o_gdn_s", [L, NS, c.AH, 128, 128]); d["o_gcv_s"] = self.dout("o_gcv_s", [L, 128, NA, NS, 3])
        d["o_ssm_s"] = self.dout("o_ssm_s", [L, NS, c.BH, 128, 64]); d["o_scv_s"] = self.dout("o_scv_s", [L, 128, NB, NS, 3])
        d["o_ret_s"] = self.dout("o_ret_s", [L, NS, c.CH, 256, 256]); d["o_fcv_s"] = self.dout("o_fcv_s", [L, 128, c.FC, NS, 2])
        d["xw"] = self.dscr("xw", [KC, 128, c.TT])
        self.r_xw = [self.R(f"xw{i}") for i in range(KC)]
        TBM = max(b[1] for b in c.blocks)
        self.TBM = TBM
        s = self.s = {}
        r = self.r = {}

        def A(name, shape, dt=F32):
            s[name] = self.sb(name, shape, dt)
            r[name] = self.R(name)
            return s[name]
        A("ident", [128, 128]); A("ones", [128, 128]); A("m64", [64, 2, 4, 64]); A("colm", [128, 8, 64]); A("rowm", [64, 8])
        A("rdt", [64, 2, c.CH, 64]); A("rqd", [128, 2, c.CH, 64]); A("rkd", [64, 2, c.CH, 8])
        A("rot", [128, 2, TBM])
        A("gain", [128, 3, KC])
        A("a_cw", [128, NA, 4]); A("b_cw", [128, NB, 4]); A("b_cb", [128, NB]); A("f_cw", [128, c.FC, 3]); A("f_cb", [128, c.FC])
        A("a_par", [64, 2, c.AH]); A("a_norm", [128, 1]); A("b_par", [64, 3, c.BH]); A("b_norm", [128, c.BDI // 128])
        A("uT", [128, KC, TBM], BF16)
        NY = max(c.AH, c.BDI // 128, c.CH * 2)
        self.NY = NY
        A("hT", [128, KC, TBM], BF16); A("yT", [128, NY, TBM], BF16)
        self.NSLOT = 2
        self.PANE = 8192
        for i in range(self.NSLOT):
            A(f"pan{i}", [128, self.PANE], BF16)
        self.pslot = 0
        self.pidx, self.player, self.cache_fill = 0, 0, True
        self.wcache, self.pending_store = {}, None
        self.r_wc = self.R("wcache")
        A("xs0", [128, TBM]); A("xs1", [128, TBM]); A("sq0", [128, TBM]); A("sq1", [128, TBM]); A("rstd", [128, TBM])
        self.xsi = 0
        A("SgS", [128, max(c.AH, c.NS), 128]); A("Ss", [128, c.BH, 64]); A("SrS", [128, max(c.CH, 8), 2, 256])
        s["Sg"], r["Sg"] = s["SgS"][:, 0:c.AH, :], r["SgS"]
        s["Ssm"], r["Ssm"] = s["SgS"][:, 0:c.NS, :], r["SgS"]
        s["Sr"], r["Sr"] = s["SrS"][:, 0:c.CH, :, :], r["SrS"]
        s["Srm"], r["Srm"] = s["SrS"][:, 0:8, :, :], r["SrS"]
        A("gtail", [128, NA, 3]); A("btail", [128, NB, 3]); A("ftail", [128, c.FC, 2])
        self.ps = [nc.alloc_psum_tensor(f"ps{i}", [128, 512], F32) for i in range(8)]
        self.rq = [[self.R(f"ps{i}q{k}") for k in range(4)] for i in range(8)]
        self.rbank = [self.R(f"ps{i}port") for i in range(8)]
        for i in range(8):
            self.rbank[i].nosame = True
            for k in range(4):
                self.rq[i][k].port = self.rbank[i]
        for nm in ("ident", "ones", "m64", "colm", "rowm", "rdt", "rqd", "rkd"):
            self.ld(s[nm][tuple(slice(None) for _ in s[nm].shape)], d["c_" + nm], [r[nm]])
        self.ld(s["gain"][:, 2, :], d["g_fin"], [r["gain"]])
        for k in range(KC):
            self.ld(d["xw"][k], d["xT"][k], [self.r_xw[k]])

    def psr(self, b, c0=0, c1=512):
        return [self.rq[b][k] for k in range(4) if c0 < (k + 1) * 128 and c1 > k * 128]

    def panel(self, pieces):
        i = self.pslot
        self.pslot = (self.pslot + 1) % self.NSLOT
        pan, rp = self.s[f"pan{i}"], self.r[f"pan{i}"]
        idx = self.pidx
        self.pidx += 1
        off, views, shapes = 0, [], []
        for ap in pieces:
            K, n = ap.shape
            kc = K // 128
            views.append(pan[:, off:off + kc * n].rearrange("p (k n) -> p k n", n=n))
            off += kc * n
            assert off <= self.PANE
        key = (self.player, idx)
        if self.cache_fill:
            for ap, v in zip(pieces, views):
                self.P.dma("pool", v, ap.rearrange("(k p) n -> p k n", p=128), writes=[rp])
            self.flush_panel_store()
            wc = self.nc.dram_tensor(f"wc_{self.player}_{idx}", [128, off], BF16).ap()
            self.wcache[key] = (wc, off)
            self.pending_store = (wc, pan, rp, off)
        else:
            wc, tot = self.wcache[key]
            assert tot == off
            self.P.dma("pool", pan[:, 0:off], wc, reads=[self.r_wc], writes=[rp])
        return rp, views

    def flush_panel_store(self):
        if self.pending_store is not None:
            wc, pan, rp, tot = self.pending_store
            self.P.dma("pool", wc, pan[:, 0:tot], reads=[rp], writes=[self.r_wc])
            self.pending_store = None

    def proj(self, view, j0, n, rhs_fn, kc_n, out, TB, rp, rrhs, wres):
        for k in range(kc_n):
            self.mm(out, view[:, k, j0:j0 + n], rhs_fn(k), k == 0, k == kc_n - 1, [rp] + list(rrhs), wres,
                    signal=(k == kc_n - 1))

    def xstage(self):
        i = self.xsi
        self.xsi ^= 1
        return self.s[f"xs{i}"], self.r[f"xs{i}"], self.s[f"sq{i}"], self.r[f"sq{i}"]

    def norm(self, gi, t0, TB, final=False):
        c, s, r, d = self.c, self.s, self.r, self.d
        psN, rN = self.ps[4], self.psr(4)
        for k in range(c.KC):
            xs, rx, sq, rs_ = self.xstage()
            self.ld(xs[:, :TB], d["xw"][k, :, t0:t0 + TB], [rx], reads=[self.r_xw[k]])
            self.act(sq[:, :TB], xs[:, :TB], AF.Square, [rx], [rs_])
            self.mm(psN[:, :TB], s["ones"][:, :], sq[:, :TB], k == 0, k == c.KC - 1, [rs_, r["ones"]], rN)
        self.act(s["rstd"][:, :TB], psN[:, :TB], AF.Ln, rN, [r["rstd"]], scale=1.0 / c.D, bias=1e-6)
        self.act(s["rstd"][:, :TB], s["rstd"][:, :TB], AF.Exp, [r["rstd"]], [r["rstd"]], scale=-0.5)
        for k in range(c.KC):
            xs, rx, sq, rs_ = self.xstage()
            self.ld(xs[:, :TB], d["xw"][k, :, t0:t0 + TB], [rx], reads=[self.r_xw[k]])
            if final:
                self.stt(sq[:, :TB], xs[:, :TB], s["gain"][:, gi, k:k + 1], s["rstd"][:, :TB], ALU.mult, ALU.mult,
                         [rx, r["gain"], r["rstd"]], [rs_])
                self.st(d["yT"][k, :, t0:t0 + TB], sq[:, :TB], [rs_], sem_res=self.out_res)
            else:
                self.stt(s["uT"][:, k, :TB], xs[:, :TB], s["gain"][:, gi, k:k + 1], s["rstd"][:, :TB], ALU.mult, ALU.mult,
                         [rx, r["gain"], r["rstd"]], [r["uT"]])

    def x_update(self, k, t0, TB, psum_ap, rps):
        xs, rx, _, _ = self.xstage()
        self.ld(xs[:, :TB], self.d["xw"][k, :, t0:t0 + TB], [rx], reads=[self.r_xw[k]])
        self.tt(xs[:, :TB], xs[:, :TB], psum_ap, ALU.add, [rx] + list(rps), [rx])
        self.st(self.d["xw"][k, :, t0:t0 + TB], xs[:, :TB], [rx], writes=[self.r_xw[k]], sem_res=rx)

    def merge(self, l, X, wb, nyc, TB, first):
        c, s, r, d = self.c, self.s, self.r, self.d
        for hc in range(c.KC):
            g0 = c.o_gates + X * c.D + hc * 128
            rp, (vg, vb) = self.panel([d["w_in"][l][:, g0:g0 + 128], wb[l][:, hc * 128:hc * 128 + 128]])
            bg, ba = (0, 1) if hc % 2 == 0 else (2, 3)
            psG, psA = self.ps[bg], self.ps[ba]
            self.proj(vg, 0, 128, lambda k: s["uT"][:, k, :TB], c.KC, psG[:, :TB], TB, rp, [r["uT"]], self.psr(bg))
            self.proj(vb, 0, 128, lambda k: s["yT"][:, k, :TB], nyc, psA[:, :TB], TB, rp, [r["yT"]], self.psr(ba))
            sq, rsq = s[f"sq{hc % 2}"], r[f"sq{hc % 2}"]
            self.act(sq[:, :TB], psG[:, :TB], AF.Sigmoid, self.psr(bg), [rsq])
            if first:
                self.tt(s["hT"][:, hc, :TB], sq[:, :TB], psA[:, :TB], ALU.mult, [rsq] + self.psr(ba), [r["hT"]])
            else:
                self.tt(sq[:, :TB], sq[:, :TB], psA[:, :TB], ALU.mult, [rsq] + self.psr(ba), [rsq])
                self.tt(s["hT"][:, hc, :TB], s["hT"][:, hc, :TB], sq[:, :TB], ALU.add, [rsq, r["hT"]], [r["hT"]])

    def out_proj(self, l, t0, TB):
        c, s, r, d = self.c, self.s, self.r, self.d
        for oc in range(0, c.KC, 2):
            n = min(2, c.KC - oc)
            rp, (v,) = self.panel([d["w_out"][l][:, oc * 128:(oc + n) * 128]])
            for j in range(n):
                b = (oc // 2 % 2) * 2 + j
                self.proj(v, j * 128, 128, lambda k: s["hT"][:, k, :TB], c.KC, self.ps[b][:, :TB], TB, rp, [r["hT"]], self.psr(b))
                self.x_update(oc + j, t0, TB, self.ps[b][:, :TB], self.psr(b))

    def ffn(self, l, bi):
        c, s, r, d = self.c, self.s, self.r, self.d
        t0, TB, nsq, sl = c.blocks[bi]
        FC = c.FC
        half = (FC + 1) // 2
        hid, rhid = s["hid"], r["hid"]
        pre, rpre = s["fpre"], r["fpre"]
        for hf in range(2):
            f0, f1 = (0, half) if hf == 0 else (half, FC)
            if f1 <= f0:
                continue
            for fc in range(f0, f1):
                rp, (vg, vu) = self.panel([d["w_fg"][l][:, fc * 128:fc * 128 + 128], d["w_fu"][l][:, fc * 128:fc * 128 + 128]])
                bg, bu = (0, 1) if fc % 2 == 0 else (2, 3)
                self.proj(vg, 0, 128, lambda k: s["uT"][:, k, :TB], c.KC, self.ps[bg][:, :TB], TB, rp, [r["uT"]], self.psr(bg))
                self.proj(vu, 0, 128, lambda k: s["uT"][:, k, :TB], c.KC, self.ps[bu][:, :TB], TB, rp, [r["uT"]], self.psr(bu))
                cv = self.conv(self.ps[bg], self.psr(bg), TB, nsq, sl, 3, pre, rpre, s["f_cw"], r["f_cw"], fc,
                               tail=(s["ftail"], r["ftail"]), st_in=d["s_fcv"][l], st_out=d["o_fcv_s"][l],
                               bias=s["f_cb"][:, fc:fc + 1], rbias=r["f_cb"], out=s["fcv"], rout=r["fcv"])
                self.tt(hid[:, fc - f0, :TB], cv, self.ps[bu][:, :TB], ALU.mult, [r["fcv"]] + self.psr(bu), [rhid])
            nk = f1 - f0
            for oc in range(c.KC):
                rp, (v,) = self.panel([d["w_fd"][l][f0 * 128:f1 * 128, oc * 128:oc * 128 + 128]])
                b = oc % 4
                self.proj(v, 0, 128, lambda k: hid[:, k, :TB], nk, self.ps[b][:, :TB], TB, rp, [rhid], self.psr(b))
                self.x_update(oc, t0, TB, self.ps[b][:, :TB], self.psr(b))

    def conv(self, ps, rps, TB, nsq, sl, W, pre, rpre, cw, rcw, ch, tail, st_in, st_out, bias, rbias, out, rout):
        s, r = self.s, self.r
        Wm = W - 1
        pv = pre[:, 0:nsq * (Wm + sl)].rearrange("p (s t) -> p s t", t=Wm + sl)
        psv = ps[:, 0:TB].rearrange("p (s t) -> p s t", t=sl)
        ov = out[:, 0:TB].rearrange("p (s t) -> p s t", t=sl)
        self.cp("act", pv[:, :, Wm:Wm + sl], psv, rps, [rpre])
        if nsq == 1:
            tl, rtl = tail
            self.cp("dve", pv[:, 0, 0:Wm], tl[:, ch, :], [rtl], [rpre])
        else:
            self.ld(pv[:, :, 0:Wm], st_in[:, ch], [rpre])
        self.ts(ov, pv[:, :, Wm:Wm + sl], cw[:, ch, Wm:Wm + 1], ALU.mult, [rpre, rcw], [rout])
        for j in range(Wm - 1, -1, -1):
            self.stt(ov, pv[:, :, j:j + sl], cw[:, ch, j:j + 1], ov, ALU.mult, ALU.add, [rpre, rcw, rout], [rout])
        if nsq == 1:
            self.cp("dve", tl[:, ch, :], pv[:, 0, sl:sl + Wm], [rpre], [rtl])
        else:
            self.st(st_out[:, ch], pv[:, :, sl:sl + Wm], [rpre], sem_res=self.out_res)
        if bias is not None:
            self.act(out[:, 0:TB], out[:, 0:TB], AF.Silu, [rout, rbias], [rout], bias=bias)
        else:
            self.act(out[:, 0:TB], out[:, 0:TB], AF.Silu, [rout], [rout])
        return out[:, 0:TB]

    def tmaj_small(self, srcT, rsrc, n, NCH, dst, rdst):
        s, r = self.s, self.r
        for ch in range(NCH):
            self.tr(self.ps[5][0:64, ch * n:(ch + 1) * n], srcT[0:n, ch * 64:(ch + 1) * 64], s["ident"][0:n, 0:n],
                    [rsrc, r["ident"]], self.psr(5, ch * n, (ch + 1) * n))
        self.cp("dve", dst[0:64, 0:NCH, 0:n], self.ps[5][0:64, 0:NCH * n].rearrange("p (c h) -> p c h", h=n),
                self.psr(5, 0, NCH * n), [rdst])

    def cum_scalars(self, kind, NCH, H, pre):
        s, r = self.s, self.r
        g, G, dec, negG = (s[pre + n] for n in ("g", "G", "dec", "negG"))
        rg, rG, rdec, rnegG = (r[pre + n] for n in ("g", "G", "dec", "negG"))
        m = s["m64"]
        n = NCH * H
        gv = g[0:64, 0:NCH, 0:H]
        self.mm(self.ps[5][0:64, 0:n], m[:, kind, 0, :], gv, True, True, [rg, r["m64"]], self.psr(5, 0, n))
        self.cp("dve", G[0:64, 0:NCH, 0:H], self.ps[5][0:64, 0:n].rearrange("p (c h) -> p c h", h=H), self.psr(5, 0, n), [rG])
        self.mm(self.ps[5][0:64, 0:n], m[:, kind, 1, :], G[0:64, 0:NCH, 0:H], True, True, [rG, r["m64"]], self.psr(5, 0, n))
        self.tt(dec[0:64, 0:NCH, 0:H], self.ps[5][0:64, 0:n].rearrange("p (c h) -> p c h", h=H), G[0:64, 0:NCH, 0:H],
                ALU.subtract, self.psr(5, 0, n) + [rG], [rdec])
        self.act(dec[0:64, 0:NCH, 0:H], dec[0:64, 0:NCH, 0:H], AF.Exp, [rdec], [rdec])
        self.ts(negG[0:64, 0:NCH, 0:H], G[0:64, 0:NCH, 0:H], -1.0, ALU.mult, [rG], [rnegG])

    def decay_mats(self, kind, Gcol, negGcol, rG, need_lo=True):
        s, r = self.s, self.r
        m = s["m64"]
        psR, rR = self.ps[5][:, 128:192], self.psr(5, 128, 192)
        self.ts(s["diag"][:, :], s["ident"][0:64, 0:64], Gcol, ALU.mult, [r["ident"]] + rG, [r["diag"]])
        self.mm(psR, s["ones"][0:64, :], s["diag"][:, :], True, True, [r["ones"], r["diag"]], rR)
        if need_lo:
            self.stt(s["Ds"][:, :], psR[0:64, :], -1.0, m[:, kind, 2, :], ALU.mult, ALU.add, rR + [r["m64"]], [r["Ds"]])
            self.act(s["Ds"][:, :], s["Ds"][:, :], AF.Exp, [r["Ds"]] + rG, [r["Ds"]], bias=Gcol)
        self.tt(s["DT"][:, :], psR[0:64, :], m[:, kind, 3, :], ALU.add, rR + [r["m64"]], [r["DT"]])
        self.act(s["DT"][:, :], s["DT"][:, :], AF.Exp, [r["DT"]] + rG, [r["DT"]], bias=negGcol)
        self.act(s["RG"][:, :], psR, AF.Exp, rR, [r["RG"]])

    def gdn(self, l, bi):
        c, s, r, d = self.c, self.s, self.r, self.d
        t0, TB, nsq, sl = c.blocks[bi]
        NCH, AH, KC = TB // 64, c.AH, c.KC
        kind = 0 if nsq == 1 else 1
        nseq = 1 if kind == 0 else 8
        csl = 64 // nseq
        m_rounds = int(np.ceil(np.log2(csl))) - 1
        H2 = 2 * AH
        uT = lambda k: s["uT"][:, k, :TB]
        ident, ones = s["ident"], s["ones"]
        rp, (v,) = self.panel([d["w_in"][l][:, c.o_ab:c.o_ab + H2]])
        self.proj(v, 0, H2, uT, KC, self.ps[4][0:H2, :TB], TB, rp, [r["uT"]], self.psr(4))
        self.cp("act", s["abT"][0:H2, :TB], self.ps[4][0:H2, :TB], self.psr(4), [r["abT"]])
        self.tmaj_small(s["abT"], r["abT"], H2, NCH, s["ab"], r["ab"])
        ab, par = s["ab"], s["a_par"]
        beta, g = s["abeta"], s["ag"]
        self.act(beta[0:64, 0:NCH, 0:AH], ab[0:64, 0:NCH, 0:AH], AF.Sigmoid, [r["ab"]], [r["abeta"]])
        for ch in range(NCH):
            self.tt(g[0:64, ch, 0:AH], ab[0:64, ch, AH:H2], par[:, 1, :], ALU.add, [r["ab"], r["a_par"]], [r["ag"]])
        self.act(g[0:64, 0:NCH, 0:AH], g[0:64, 0:NCH, 0:AH], AF.Exp, [r["ag"]], [r["ag"]])
        self.act(g[0:64, 0:NCH, 0:AH], g[0:64, 0:NCH, 0:AH], AF.Ln, [r["ag"]], [r["ag"]], bias=1.0)
        for ch in range(NCH):
            self.tt(g[0:64, ch, 0:AH], g[0:64, ch, 0:AH], par[:, 0, :], ALU.mult, [r["ag"], r["a_par"]], [r["ag"]])
        self.cum_scalars(kind, NCH, AH, "a")
        G, dec, negG = s["aG"], s["adec"], s["anegG"]
        nbeta, bg = s["anbeta"], s["abg"]
        self.ts(nbeta[0:64, 0:NCH, 0:AH], beta[0:64, 0:NCH, 0:AH], -1.0, ALU.mult, [r["abeta"]], [r["anbeta"]])
        self.act(bg[0:64, 0:NCH, 0:AH], G[0:64, 0:NCH, 0:AH], AF.Exp, [r["aG"]], [r["abg"]])
        self.tt(bg[0:64, 0:NCH, 0:AH], bg[0:64, 0:NCH, 0:AH], beta[0:64, 0:NCH, 0:AH], ALU.mult, [r["abg"], r["abeta"]], [r["abg"]])
        rsc = [r["aG"], r["adec"], r["anegG"], r["anbeta"], r["abg"], r["abeta"]]
        if self.gstop == "scalars":
            return

        pre, rpre = s["pre"], r["pre"]
        cv, rcv = s["cv"], r["cv"]
        Wc = 3 + sl
        for h in range(AH):
            o = c.o_aqkv
            rp1, (vq, vk) = self.panel([d["w_in"][l][:, o + h * 128:o + h * 128 + 128],
                                        d["w_in"][l][:, o + (AH + h) * 128:o + (AH + h) * 128 + 128]])
            rp2, (vv, vz) = self.panel([d["w_in"][l][:, o + (2 * AH + h) * 128:o + (2 * AH + h) * 128 + 128],
                                        d["w_in"][l][:, c.o_az + h * 128:c.o_az + h * 128 + 128]])
            for b, (vw, rpp) in enumerate(((vq, rp1), (vk, rp1), (vv, rp2), (vz, rp2))):
                self.proj(vw, 0, 128, uT, KC, self.ps[b][:, :TB], TB, rpp, [r["uT"]], self.psr(b))
            if self.gstop == "proj":
                return
            for st in range(3):
                self.conv(self.ps[st], self.psr(st), TB, nsq, sl, 4, pre[:, st, :], rpre, s["a_cw"], r["a_cw"], st * AH + h,
                          tail=(s["gtail"], r["gtail"]), st_in=d["s_gcv"][l], st_out=d["o_gcv_s"][l],
                          bias=None, rbias=None, out=cv[:, st, :], rout=rcv)
            self.act(s["zs"][:, :TB], self.ps[3][:, :TB], AF.Silu, self.psr(3), [r["zs"]])
            if self.gstop == "conv":
                return
            for st in range(2):
                self.act(s["sqb"][:, :TB], cv[:, st, :TB], AF.Square, [rcv], [r["sqb"]])
                self.mm(self.ps[4][:, :TB], ones[:, :], s["sqb"][:, :TB], True, True, [r["ones"], r["sqb"]], self.psr(4))
                self.act(s["sqb"][:, :TB], self.ps[4][:, :TB], AF.Ln, self.psr(4), [r["sqb"]], bias=1e-6)
                self.act(s["sqb"][:, :TB], s["sqb"][:, :TB], AF.Exp, [r["sqb"]], [r["sqb"]], scale=-0.5,
                         bias=(float(np.log(128.0 ** -0.5)) if st == 0 else 0.0))
                self.tt(cv[:, st, :TB], cv[:, st, :TB], s["sqb"][:, :TB], ALU.mult, [rcv, r["sqb"]], [rcv])
            self.dump(f"gdn_q{h}", cv[:, 0, :TB], rcv, [128, TB]); self.dump(f"gdn_k{h}", cv[:, 1, :TB], rcv, [128, TB])
            self.dump(f"gdn_v{h}", cv[:, 2, :TB], rcv, [128, TB])
            if self.gstop == "l2":
                return
            for ch in range(NCH):
                for j, st in enumerate((1, 2)):
                    self.tr(self.ps[4][0:64, j * 128:(j + 1) * 128], cv[:, st, ch * 64:(ch + 1) * 64], ident[:, :],
                            [rcv, r["ident"]], self.psr(4, j * 128, (j + 1) * 128))
                self.cp("act", s["kvT"][0:64, ch, :], self.ps[4][0:64, 0:256], self.psr(4, 0, 256), [r["kvT"]])
            if self.gstop == "tmaj":
                return
            if kind == 1:
                self.ld(s["Ssm"][:, 0:c.NS, 0:128], d["s_gdn"][l, :, h].rearrange("s k v -> k s v"), [r["Ssm"]])
            for ch in range(NCH):
                cs_ = slice(ch * 64, ch * 64 + 64)
                qT, kT = cv[:, 0, cs_], cv[:, 1, cs_]
                kTm, vTm = s["kvT"][0:64, ch, 0:128], s["kvT"][0:64, ch, 128:256]
                col = lambda t: t[0:64, ch, h:h + 1]
                psM = self.ps[5]
                self.mm(psM[0:64, 0:64], kT, kT, True, True, [rcv], self.psr(5, 0, 64))
                self.mm(psM[0:64, 64:128], kT, qT, True, True, [rcv], self.psr(5, 64, 128))
                self.decay_mats(kind, col(G), col(negG), [r["aG"], r["anegG"]])
                Pm, PTm, TTm = s["Pm"], s["PTm"], s["TTm"]
                if self.gstop == "dmat":
                    return
                self.stt(Pm[:, :], psM[0:64, 0:64], col(nbeta), s["Ds"][:, :], ALU.mult, ALU.mult,
                         self.psr(5, 0, 64) + [r["anbeta"], r["Ds"]], [r["Pm"]])
                self.tt(s["QKm"][:, :], psM[0:64, 64:128], s["DT"][:, :], ALU.mult, self.psr(5, 64, 128) + [r["DT"]], [r["QKm"]])
                if self.gstop == "pq":
                    return
                self.tr(psM[0:64, 256:320], Pm[:, :], ident[0:64, 0:64], [r["Pm"], r["ident"]], self.psr(5, 256, 320))
                self.cp("act", PTm[:, :], psM[0:64, 256:320], self.psr(5, 256, 320), [r["PTm"]])
                self.tt(TTm[:, :], psM[0:64, 256:320], ident[0:64, 0:64], ALU.add, self.psr(5, 256, 320) + [r["ident"]], [r["TTm"]])
                if self.gstop == "ptr":
                    return
                for k in range(m_rounds):
                    last = k == m_rounds - 1
                    if not last:
                        self.mm(psM[0:64, 256:320], Pm[:, :], PTm[:, :], True, True, [r["Pm"], r["PTm"]], self.psr(5, 256, 320))
                    self.mm(psM[0:64, 320:384], PTm[:, :], Pm[:, :], True, True, [r["Pm"], r["PTm"]], self.psr(5, 320, 384))
                    if self.gstop == "r_mm":
                        return
                    if not last:
                        self.cp("act", PTm[:, :], psM[0:64, 256:320], self.psr(5, 256, 320), [r["PTm"]])
                    self.cp("dve", Pm[:, :], psM[0:64, 320:384], self.psr(5, 320, 384), [r["Pm"]])
                    if self.gstop == "r_ev":
                        return
                    self.mm(psM[0:64, 384:448], Pm[:, :], TTm[:, :], True, True, [r["Pm"], r["TTm"]], self.psr(5, 384, 448))
                    if self.gstop == "r_mm3":
                        return
                    self.tt(TTm[:, :], psM[0:64, 384:448], TTm[:, :], ALU.add, [r["TTm"]] + self.psr(5, 384, 448), [r["TTm"]])
                if self.gstop == "neumann":
                    return
                self.ts(s["vb"][:, :], vTm, col(beta), ALU.mult, [r["kvT"], r["abeta"]], [r["vb"]])
                self.ts(s["kbg"][:, :], kTm, col(bg), ALU.mult, [r["kvT"], r["abg"]], [r["kbg"]])
                psW = self.ps[6][:, 384:448]
                self.mm(psW, s["kbg"][:, :], TTm[:, :], True, True, [r["kbg"], r["TTm"]], self.psr(6, 384, 448))
                self.tt(s["qg"][:, 0, :], qT, s["RG"][:, :], ALU.mult, [rcv, r["RG"]], [r["qg"]])
                if nseq == 1:
                    self.ts(s["wTn"][:, 0, :], psW, -1.0, ALU.mult, self.psr(6, 384, 448), [r["wTn"]])
                else:
                    for sq_ in range(nseq - 1, -1, -1):
                        self.stt(s["wTn"][:, sq_, :], psW, -1.0, s["colm"][:, sq_, :], ALU.mult, ALU.mult,
                                 self.psr(6, 384, 448) + [r["colm"]], [r["wTn"]])
                        self.tt(s["qg"][:, sq_, :], s["qg"][:, 0, :], s["colm"][:, sq_, :], ALU.mult, [r["qg"], r["colm"]], [r["qg"]])
                if nseq == 1:
                    self.ts(s["kdec"][:, 0, :], kTm, col(dec), ALU.mult, [r["kvT"], r["adec"]], [r["kdec"]])
                else:
                    self.ts(s["dm"][:, :], s["rowm"][:, :], col(dec), ALU.mult, [r["rowm"], r["adec"]], [r["dm"]])
                    for sq_ in range(nseq):
                        self.ts(s["kdec"][:, sq_, :], kTm, s["dm"][:, sq_:sq_ + 1], ALU.mult, [r["kvT"], r["dm"]], [r["kdec"]])
                Sv = (lambda q_: s["Sg"][:, h, :]) if kind == 0 else (lambda q_: s["Ssm"][:, ch * 8 + q_, 0:128])
                rS = r["Sg"] if kind == 0 else r["Ssm"]
                psVN, rVN = self.ps[6][0:64, 0:128], self.psr(6, 0, 128)
                self.mm(psVN, TTm[:, :], s["vb"][:, :], True, False, [r["TTm"], r["vb"]], rVN)
                for sq_ in range(nseq):
                    self.mm(psVN, s["wTn"][:, sq_, :], Sv(sq_), False, sq_ == nseq - 1, [r["wTn"], rS], rVN)
                self.cp("act", s["vn"][:, :], psVN, rVN, [r["vn"]])
                if self.gstop == "vn":
                    return
                psO, rO = self.ps[6][0:64, 128:256], self.psr(6, 128, 256)
                for sq_ in range(nseq):
                    self.mm(psO, s["qg"][:, sq_, :], Sv(sq_), sq_ == 0, False, [r["qg"], rS], rO)
                self.mm(psO, s["QKm"][:, :], s["vn"][:, :], False, True, [r["QKm"], r["vn"]], rO)
                for sq_ in range(nseq):
                    psS, rSS = self.ps[6][:, 256:384], self.psr(6, 256, 384)
                    self.mm(psS, s["kdec"][:, sq_, :], s["vn"][:, :], True, True, [r["kdec"], r["vn"]], rSS)
                    lastc = (sq_ + 1) * csl - 1
                    self.stt(Sv(sq_), Sv(sq_), s["RG"][:, lastc:lastc + 1], psS, ALU.mult, ALU.add, [rS, r["RG"]] + rSS, [rS])
                self.act(s["on"][:, :], psO, AF.Square, rO, [r["on"]])
                self.rsum(s["ssq"][:, 0:1], s["on"][:, :], [r["on"]], [r["ssq"]])
                self.act(s["ssq"][:, 0:1], s["ssq"][:, 0:1], AF.Ln, [r["ssq"]], [r["ssq"]], scale=1.0 / 128, bias=1e-6)
                self.act(s["ssq"][:, 0:1], s["ssq"][:, 0:1], AF.Exp, [r["ssq"]], [r["ssq"]], scale=-0.5)
                self.ts(s["on"][:, :], psO, s["ssq"][:, 0:1], ALU.mult, rO + [r["ssq"]], [r["on"]])
                self.tr(self.ps[7][:, ch * 64:(ch + 1) * 64], s["on"][:, :], ident[0:64, 0:64], [r["on"], r["ident"]],
                        self.psr(7, ch * 64, (ch + 1) * 64))
            self.stt(s["yT"][:, h, :TB], self.ps[7][:, :TB], s["a_norm"][:, 0:1], s["zs"][:, :TB], ALU.mult, ALU.mult,
                     self.psr(7) + [r["a_norm"], r["zs"]], [r["yT"]])
            if kind == 1:
                self.st(d["o_gdn_s"][l, :, h].rearrange("s k v -> k s v"), s["Ssm"][:, 0:c.NS, 0:128], [r["Ssm"]], sem_res=self.out_res)

    def ssd(self, l, bi):
        c, s, r, d = self.c, self.s, self.r, self.d
        t0, TB, nsq, sl = c.blocks[bi]
        NCH, BH, KC, BG, REP = TB // 64, c.BH, c.KC, c.BG, c.BREP
        kind = 0 if nsq == 1 else 1
        nseq = 1 if kind == 0 else 8
        csl = 64 // nseq
        uT = lambda k: s["uT"][:, k, :TB]
        ident, ones = s["ident"], s["ones"]
        NXC = c.BDI // 128
        rp, (v,) = self.panel([d["w_in"][l][:, c.o_bdt:c.o_bdt + BH]])
        self.proj(v, 0, BH, uT, KC, self.ps[4][0:BH, :TB], TB, rp, [r["uT"]], self.psr(4))
        self.cp("act", s["abT"][0:BH, :TB], self.ps[4][0:BH, :TB], self.psr(4), [r["abT"]])
        self.tmaj_small(s["abT"], r["abT"], BH, NCH, s["bdt"], r["bdt"])
        dt, g, par = s["bdt"], s["bg"], s["b_par"]
        for ch in range(NCH):
            self.tt(dt[0:64, ch, 0:BH], dt[0:64, ch, 0:BH], par[:, 1, :], ALU.add, [r["bdt"], r["b_par"]], [r["bdt"]])
        self.act(dt[0:64, 0:NCH, 0:BH], dt[0:64, 0:NCH, 0:BH], AF.Exp, [r["bdt"]], [r["bdt"]])
        self.act(dt[0:64, 0:NCH, 0:BH], dt[0:64, 0:NCH, 0:BH], AF.Ln, [r["bdt"]], [r["bdt"]], bias=1.0)
        for ch in range(NCH):
            self.tt(g[0:64, ch, 0:BH], dt[0:64, ch, 0:BH], par[:, 0, :], ALU.mult, [r["bdt"], r["b_par"]], [r["bg"]])
        self.cum_scalars(kind, NCH, BH, "b")
        G, dec, negG = s["bG"], s["bdec"], s["bnegG"]
        pre, rpre = s["pre"], r["pre"]
        cv, rcv = s["cv"], r["cv"]
        o_x, o_B, o_C = c.o_bxbc, c.o_bxbc + c.BDI, c.o_bxbc + c.BDI + BG * 128
        XG = REP * 64 // 128
        for gi in range(BG):
            rp, (vB, vC) = self.panel([d["w_in"][l][:, o_B + gi * 128:o_B + gi * 128 + 128],
                                       d["w_in"][l][:, o_C + gi * 128:o_C + gi * 128 + 128]])
            self.proj(vB, 0, 128, uT, KC, self.ps[0][:, :TB], TB, rp, [r["uT"]], self.psr(0))
            self.proj(vC, 0, 128, uT, KC, self.ps[1][:, :TB], TB, rp, [r["uT"]], self.psr(1))
            for st, chn in ((0, NXC + gi), (1, NXC + BG + gi)):
                self.conv(self.ps[st], self.psr(st), TB, nsq, sl, 4, pre[:, st, :], rpre, s["b_cw"], r["b_cw"], chn,
                          tail=(s["btail"], r["btail"]), st_in=d["s_scv"][l], st_out=d["o_scv_s"][l],
                          bias=s["b_cb"][:, chn:chn + 1], rbias=r["b_cb"], out=cv[:, st, :], rout=rcv)
            for ch in range(NCH):
                self.tr(self.ps[4][0:64, 0:128], cv[:, 0, ch * 64:(ch + 1) * 64], ident[:, :], [rcv, r["ident"]], self.psr(4, 0, 128))
                self.cp("act", s["kvT"][0:64, ch, 0:128], self.ps[4][0:64, 0:128], self.psr(4, 0, 128), [r["kvT"]])
            for ch in range(NCH):
                cs_ = slice(ch * 64, ch * 64 + 64)
                self.mm(self.ps[5][0:64, 256:320], cv[:, 0, cs_], cv[:, 1, cs_], True, True, [rcv], self.psr(5, 256, 320))
                self.cp("act", s["cbT"][:, ch, :], self.ps[5][0:64, 256:320], self.psr(5, 256, 320), [r["cbT"]])
            for xc in range(XG):
                fch = gi * XG + xc
                rp, (vx, vz) = self.panel([d["w_in"][l][:, o_x + fch * 128:o_x + fch * 128 + 128],
                                           d["w_in"][l][:, c.o_bz + fch * 128:c.o_bz + fch * 128 + 128]])
                self.proj(vx, 0, 128, uT, KC, self.ps[2][:, :TB], TB, rp, [r["uT"]], self.psr(2))
                self.proj(vz, 0, 128, uT, KC, self.ps[3][:, :TB], TB, rp, [r["uT"]], self.psr(3))
                self.conv(self.ps[2], self.psr(2), TB, nsq, sl, 4, pre[:, 2, :], rpre, s["b_cw"], r["b_cw"], fch,
                          tail=(s["btail"], r["btail"]), st_in=d["s_scv"][l], st_out=d["o_scv_s"][l],
                          bias=s["b_cb"][:, fch:fch + 1], rbias=r["b_cb"], out=cv[:, 2, :], rout=rcv)
                self.act(s["zs"][:, :TB], self.ps[3][:, :TB], AF.Silu, self.psr(3), [r["zs"]])
                for ch in range(NCH):
                    self.tr(self.ps[4][0:64, 128:256], cv[:, 2, ch * 64:(ch + 1) * 64], ident[:, :], [rcv, r["ident"]], self.psr(4, 128, 256))
                    self.cp("act", s["kvT"][0:64, ch, 128:256], self.ps[4][0:64, 128:256], self.psr(4, 128, 256), [r["kvT"]])
                for hh in range(2):
                    h = fch * 2 + hh
                    if kind == 1:
                        self.ld(s["Ssm"][:, 0:c.NS, 0:64], d["s_ssm"][l, :, h].rearrange("s n p -> n s p"), [r["Ssm"]])
                    for ch in range(NCH):
                        cs_ = slice(ch * 64, ch * 64 + 64)
                        CT = cv[:, 1, cs_]
                        Bm = s["kvT"][0:64, ch, 0:128]
                        Xm = s["kvT"][0:64, ch, 128 + hh * 64:128 + hh * 64 + 64]
                        col = lambda t: t[0:64, ch, h:h + 1]
                        self.decay_mats(kind, col(G), col(negG), [r["bG"], r["bnegG"]], need_lo=False)
                        self.tt(s["QKm"][:, :], s["cbT"][:, ch, :], s["DT"][:, :], ALU.mult, [r["cbT"], r["DT"]], [r["QKm"]])
                        self.ts(s["vb"][:, 0:64], Xm, col(dt), ALU.mult, [r["kvT"], r["bdt"]], [r["vb"]])
                        self.tt(s["qg"][:, 0, :], CT, s["RG"][:, :], ALU.mult, [rcv, r["RG"]], [r["qg"]])
                        if nseq == 1:
                            self.ts(s["kdec"][:, 0, :], Bm, col(dec), ALU.mult, [r["kvT"], r["bdec"]], [r["kdec"]])
                        else:
                            for sq_ in range(nseq - 1, -1, -1):
                                self.tt(s["qg"][:, sq_, :], s["qg"][:, 0, :], s["colm"][:, sq_, :], ALU.mult, [r["qg"], r["colm"]], [r["qg"]])
                            self.ts(s["dm"][:, :], s["rowm"][:, :], col(dec), ALU.mult, [r["rowm"], r["bdec"]], [r["dm"]])
                            for sq_ in range(nseq):
                                self.ts(s["kdec"][:, sq_, :], Bm, s["dm"][:, sq_:sq_ + 1], ALU.mult, [r["kvT"], r["dm"]], [r["kdec"]])
                        Sv = (lambda q_: s["Ss"][:, h, :]) if kind == 0 else (lambda q_: s["Ssm"][:, ch * 8 + q_, 0:64])
                        rS = r["Ss"] if kind == 0 else r["Ssm"]
                        psO, rO = self.ps[6][0:64, 128:192], self.psr(6, 128, 192)
                        for sq_ in range(nseq):
                            self.mm(psO, s["qg"][:, sq_, :], Sv(sq_), sq_ == 0, False, [r["qg"], rS], rO)
                        self.mm(psO, s["QKm"][:, :], s["vb"][:, 0:64], False, True, [r["QKm"], r["vb"]], rO)
                        for sq_ in range(nseq):
                            psS, rSS = self.ps[6][:, 256:320], self.psr(6, 256, 320)
                            self.mm(psS, s["kdec"][:, sq_, :], s["vb"][:, 0:64], True, True, [r["kdec"], r["vb"]], rSS)
                            lastc = (sq_ + 1) * csl - 1
                            self.stt(Sv(sq_), Sv(sq_), s["RG"][:, lastc:lastc + 1], psS, ALU.mult, ALU.add, [rS, r["RG"]] + rSS, [rS])
                        self.stt(s["onb"][:, ch, hh * 64:hh * 64 + 64], Xm, par[:, 2, h:h + 1], psO, ALU.mult, ALU.add,
                                 [r["kvT"], r["b_par"]] + rO, [r["onb"]])
                        if hh == 1:
                            self.tr(self.ps[7][:, ch * 64:(ch + 1) * 64], s["onb"][:, ch, :], ident[0:64, 0:64], [r["onb"], r["ident"]],
                                    self.psr(7, ch * 64, (ch + 1) * 64))
                    if kind == 1:
                        self.st(d["o_ssm_s"][l, :, h].rearrange("s n p -> n s p"), s["Ssm"][:, 0:c.NS, 0:64], [r["Ssm"]], sem_res=self.out_res)
                self.tt(s["ybT"][:, xc, :TB], self.ps[7][:, :TB], s["zs"][:, :TB], ALU.mult, self.psr(7) + [r["zs"]], [r["ybT"]])
            for xc in range(XG):
                self.act(s["sqb"][:, :TB], s["ybT"][:, xc, :TB], AF.Square, [r["ybT"]], [r["sqb"]])
                self.mm(self.ps[4][:, :TB], ones[:, :], s["sqb"][:, :TB], xc == 0, xc == XG - 1, [r["ones"], r["sqb"]], self.psr(4))
            self.act(s["sqb"][:, :TB], self.ps[4][:, :TB], AF.Ln, self.psr(4), [r["sqb"]], scale=1.0 / (XG * 128), bias=1e-6)
            self.act(s["sqb"][:, :TB], s["sqb"][:, :TB], AF.Exp, [r["sqb"]], [r["sqb"]], scale=-0.5)
            for xc in range(XG):
                fch = gi * XG + xc
                self.stt(s["yT"][:, fch, :TB], s["ybT"][:, xc, :TB], s["b_norm"][:, fch:fch + 1], s["sqb"][:, :TB], ALU.mult, ALU.mult,
                         [r["ybT"], r["b_norm"], r["sqb"]], [r["yT"]])

    def ret(self, l, bi):
        c, s, r, d = self.c, self.s, self.r, self.d
        t0, TB, nsq, sl = c.blocks[bi]
        NCH, CH, KC = TB // 64, c.CH, c.KC
        kind = 0 if nsq == 1 else 1
        nseq = 1 if kind == 0 else 8
        csl = 64 // nseq
        uT = lambda k: s["uT"][:, k, :TB]
        ident = s["ident"]
        cv, rcv = s["cv"], r["cv"]
        cos, sin = s["rot"][:, 0, :TB], s["rot"][:, 1, :TB]
        for h in range(CH):
            gam = 1.0 - 2.0 ** (-5.0 - h)
            for which, o0, base, scl in ((0, c.o_cq, 0, 1.0), (1, c.o_ck, 2, 1.0 / 16.0)):
                rp, (vw,) = self.panel([d["w_in"][l][:, o0 + h * 256:o0 + h * 256 + 256]])
                b1, b2 = (0, 1) if which == 0 else (2, 3)
                self.proj(vw, 0, 128, uT, KC, self.ps[b1][:, :TB], TB, rp, [r["uT"]], self.psr(b1))
                self.proj(vw, 128, 128, uT, KC, self.ps[b2][:, :TB], TB, rp, [r["uT"]], self.psr(b2))
                t1, t2 = self.ps[b1][:, :TB], self.ps[b2][:, :TB]
                o1, o2 = cv[:, base, :TB], cv[:, base + 1, :TB]
                ta, tb = s["sqb"][:, :TB], s["zs"][:, :TB]
                self.stt(o1, t1, scl, cos, ALU.mult, ALU.mult, self.psr(b1) + [r["rot"]], [rcv])
                self.stt(ta, t2, scl, sin, ALU.mult, ALU.mult, self.psr(b2) + [r["rot"]], [r["sqb"]])
                self.tt(o1, o1, ta, ALU.subtract, [rcv, r["sqb"]], [rcv])
                self.stt(o2, t1, scl, sin, ALU.mult, ALU.mult, self.psr(b1) + [r["rot"]], [rcv])
                self.stt(tb, t2, scl, cos, ALU.mult, ALU.mult, self.psr(b2) + [r["rot"]], [r["zs"]])
                self.tt(o2, o2, tb, ALU.add, [rcv, r["zs"]], [rcv])
            rp, (vv,) = self.panel([d["w_in"][l][:, c.o_cv + h * 256:c.o_cv + h * 256 + 256]])
            self.proj(vv, 0, 128, uT, KC, self.ps[0][:, :TB], TB, rp, [r["uT"]], self.psr(0))
            self.proj(vv, 128, 128, uT, KC, self.ps[1][:, :TB], TB, rp, [r["uT"]], self.psr(1))
            for j in range(2):
                self.cp("act", s["ybT"][:, j, :TB], self.ps[j][:, :TB], self.psr(j), [r["ybT"]])
            rp, (vg,) = self.panel([d["w_in"][l][:, c.o_cg + h * 256:c.o_cg + h * 256 + 256]])
            self.proj(vg, 0, 128, uT, KC, self.ps[2][:, :TB], TB, rp, [r["uT"]], self.psr(2))
            self.proj(vg, 128, 128, uT, KC, self.ps[3][:, :TB], TB, rp, [r["uT"]], self.psr(3))
            for j in range(2):
                self.act(s["gs"][:, j, :TB], self.ps[2 + j][:, :TB], AF.Silu, self.psr(2 + j), [r["gs"]])
            for ch in range(NCH):
                for j in range(2):
                    self.tr(self.ps[4][0:64, j * 128:(j + 1) * 128], cv[:, 2 + j, ch * 64:(ch + 1) * 64], ident[:, :],
                            [rcv, r["ident"]], self.psr(4, j * 128, (j + 1) * 128))
                    self.tr(self.ps[4][0:64, 256 + j * 128:256 + (j + 1) * 128], s["ybT"][:, j, ch * 64:(ch + 1) * 64], ident[:, :],
                            [r["ybT"], r["ident"]], self.psr(4, 256 + j * 128, 256 + (j + 1) * 128))
                self.cp("act", s["kvT"][0:64, ch, :], self.ps[4][0:64, 0:256], self.psr(4, 0, 256), [r["kvT"]])
                self.cp("dve", s["vT2"][0:64, ch, :], self.ps[4][0:64, 256:512], self.psr(4, 256, 512), [r["vT2"]])
            for ch in range(NCH):
                cs_ = slice(ch * 64, ch * 64 + 64)
                if kind == 1:
                    for kc in range(2):
                        self.ld(s["Srm"][:, :, kc, :], d["s_ret"][l, ch * 8:ch * 8 + 8, h, kc * 128:(kc + 1) * 128, :].rearrange("s p v -> p s v"), [r["Srm"]])
                psM = self.ps[5]
                for kc in range(2):
                    self.mm(psM[0:64, 0:64], cv[:, 2 + kc, cs_], cv[:, kc, cs_], kc == 0, kc == 1, [rcv], self.psr(5, 0, 64))
                self.tt(s["QKm"][:, :], psM[0:64, 0:64], s["rdt"][:, kind, h, :], ALU.mult, self.psr(5, 0, 64) + [r["rdt"]], [r["QKm"]])
                Vm = s["vT2"][0:64, ch, :]
                for kc in range(2):
                    if nseq == 1:
                        self.tt(s["qg"][:, kc * 8, :], cv[:, kc, cs_], s["rqd"][:, kind, h, :], ALU.mult, [rcv, r["rqd"]], [r["qg"]])
                        self.ts(s["kdec"][:, kc * 8, :], s["kvT"][0:64, ch, kc * 128:(kc + 1) * 128], s["rkd"][:, kind, h, 0:1], ALU.mult,
                                [r["kvT"], r["rkd"]], [r["kdec"]])
                    else:
                        self.tt(s["sqb"][:, 0:64], cv[:, kc, cs_], s["rqd"][:, kind, h, :], ALU.mult, [rcv, r["rqd"]], [r["sqb"]])
                        for sq_ in range(nseq):
                            self.tt(s["qg"][:, kc * 8 + sq_, :], s["sqb"][:, 0:64], s["colm"][:, sq_, :], ALU.mult, [r["sqb"], r["colm"]], [r["qg"]])
                            self.ts(s["kdec"][:, kc * 8 + sq_, :], s["kvT"][0:64, ch, kc * 128:(kc + 1) * 128], s["rkd"][:, kind, h, sq_:sq_ + 1],
                                    ALU.mult, [r["kvT"], r["rkd"]], [r["kdec"]])
                Sv = (lambda kc, q_: s["Sr"][:, h, kc, :]) if kind == 0 else (lambda kc, q_: s["Srm"][:, q_, kc, :])
                rS = r["Sr"] if kind == 0 else r["Srm"]
                psO, rO = self.ps[6][0:64, 0:256], self.psr(6, 0, 256)
                first = True
                for kc in range(2):
                    for sq_ in range(nseq):
                        self.mm(psO, s["qg"][:, kc * 8 + sq_, :], Sv(kc, sq_), first, False, [r["qg"], rS], rO)
                        first = False
                self.mm(psO, s["QKm"][:, :], Vm, False, True, [r["QKm"], r["vT2"]], rO)
                for kc in range(2):
                    for sq_ in range(nseq):
                        psS, rSS = self.ps[6][:, 256:512], self.psr(6, 256, 512)
                        self.mm(psS, s["kdec"][:, kc * 8 + sq_, :], Vm, True, True, [r["kdec"], r["vT2"]], rSS)
                        self.stt(Sv(kc, sq_), Sv(kc, sq_), float(gam ** csl), psS, ALU.mult, ALU.add, [rS] + rSS, [rS])
                if kind == 1:
                    for kc in range(2):
                        self.st(d["o_ret_s"][l, ch * 8:ch * 8 + 8, h, kc * 128:(kc + 1) * 128, :].rearrange("s p v -> p s v"), s["Srm"][:, :, kc, :], [r["Srm"]],
                                sem_res=self.out_res)
                self.act(s["on2"][:, :], psO, AF.Square, rO, [r["on2"]])
                self.rsum(s["ssq"][:, 0:1], s["on2"][:, :], [r["on2"]], [r["ssq"]])
                self.act(s["ssq"][:, 0:1], s["ssq"][:, 0:1], AF.Ln, [r["ssq"]], [r["ssq"]], scale=1.0 / 256, bias=1e-6)
                self.act(s["ssq"][:, 0:1], s["ssq"][:, 0:1], AF.Exp, [r["ssq"]], [r["ssq"]], scale=-0.5)
                self.ts(s["on2"][:, :], psO, s["ssq"][:, 0:1], ALU.mult, rO + [r["ssq"]], [r["on2"]])
                for j, pb in enumerate((7, 3)):
                    self.tr(self.ps[pb][:, ch * 64:(ch + 1) * 64], s["on2"][:, j * 128:(j + 1) * 128], ident[0:64, 0:64],
                            [r["on2"], r["ident"]], self.psr(pb, ch * 64, (ch + 1) * 64))
            for j, pb in enumerate((7, 3)):
                self.tt(s["yT"][:, 2 * h + j, :TB], self.ps[pb][:, :TB], s["gs"][:, j, :TB], ALU.mult, self.psr(pb) + [r["gs"]], [r["yT"]])

    def alloc_mixer(self):
        c, s, r = self.c, self.s, self.r
        TBM = self.TBM
        NCHM = TBM // 64

        def A(name, shape, dt=F32):
            s[name] = self.sb(name, shape, dt)
            r[name] = self.R(name)
        HM = max(2 * c.AH, c.BH)
        assert HM <= 64
        A("abT", [64, TBM])
        for n in ("ab", "abeta", "ag", "aG", "adec", "anegG", "anbeta", "abg", "bdt", "bg", "bG", "bdec", "bnegG"):
            A(n, [64, NCHM, HM])
        A("pre", [128, 3, max(TBM + 8, c.NS * 11 + 8)]); A("cv", [128, 4, TBM]); A("zs", [128, TBM]); A("sqb", [128, TBM]); A("gs", [128, 2, TBM])
        A("cbT", [64, NCHM, 64])
        half = (c.FC + 1) // 2
        XGm = max(2, c.BREP * 64 // 128)
        mix_words = 2048 + 1024 + 2 * NCHM * 256 + XGm * TBM
        FPRE = max(TBM + 8, c.NS * 10 + 8)
        ffn_words = (half * TBM + 1) // 2 + FPRE + TBM
        s["ovl"] = self.sb("ovl", [128, max(mix_words, ffn_words)])
        ovl = s["ovl"]

        def V(name, npart, shape, off):
            n = int(np.prod(shape))
            v = ovl[0:npart, off:off + n]
            if len(shape) == 2:
                v = v.rearrange("p (a b) -> p a b", b=shape[1])
            elif len(shape) == 3:
                v = v.rearrange("p (a b c) -> p a b c", b=shape[1], c=shape[2])
            s[name] = v
            r[name] = self.R(name)
            return off + n
        o_ = 0
        o_ = V("kdec", 64, [16, 128], o_); o_ = V("qg", 128, [16, 64], o_)
        o_ = V("kvT", 64, [NCHM, 256], o_); o_ = V("vT2", 64, [NCHM, 256], o_); o_ = V("ybT", 128, [XGm, TBM], o_)
        hw = (half * TBM + 1) // 2
        s["hid"] = ovl[:, 0:hw].bitcast(BF16)[:, 0:half * TBM].rearrange("p (k t) -> p k t", t=TBM)
        r["hid"] = self.R("hid")
        s["fpre"], r["fpre"] = ovl[:, hw:hw + FPRE], self.R("fpre")
        s["fcv"], r["fcv"] = ovl[:, hw + FPRE:hw + FPRE + TBM], self.R("fcv")
        A("diag", [64, 64]); A("Ds", [64, 64]); A("DT", [64, 64]); A("RG", [128, 64])
        A("Pm", [64, 64]); A("PTm", [64, 64]); A("TTm", [64, 64]); A("QKm", [64, 64])
        A("vb", [64, 128]); A("kbg", [64, 128]); A("vn", [64, 128]); A("on", [64, 128]); A("on2", [64, 256]); A("ssq", [64, 1]); A("onb", [64, NCHM, 128])
        A("wTn", [128, 8, 64]); A("dm", [64, 8])

    def layer(self, l):
        c, s, r, d = self.c, self.s, self.r, self.d
        self.ld(s["gain"][:, 0, :], d["g_mix"][l], [r["gain"]]); self.ld(s["gain"][:, 1, :], d["g_ffn"][l], [r["gain"]])
        for nm in ("a_cw", "b_cw", "b_cb", "f_cw", "f_cb", "a_norm", "b_norm"):
            self.ld(s[nm][tuple(slice(None) for _ in s[nm].shape)], d[nm][l], [r[nm]])
        self.ld(s["a_par"][:, 0, :], d["a_alog"][l].partition_broadcast(64), [r["a_par"]])
        self.ld(s["a_par"][:, 1, :], d["a_dtb"][l].partition_broadcast(64), [r["a_par"]])
        self.ld(s["b_par"][:, 0, :], d["b_alog"][l].partition_broadcast(64), [r["b_par"]])
        self.ld(s["b_par"][:, 1, :], d["b_dtb"][l].partition_broadcast(64), [r["b_par"]])
        self.ld(s["b_par"][:, 2, :], d["b_d"][l].partition_broadcast(64), [r["b_par"]])
        for nm in ("a_par", "b_par"):
            self.act(s[nm][:, 0, :], s[nm][:, 0, :], AF.Exp, [r[nm]], [r[nm]])
            self.ts(s[nm][:, 0, :], s[nm][:, 0, :], -1.0, ALU.mult, [r[nm]], [r[nm]])
        self.memset("dve", s["Sg"], 0.0, [r["Sg"]]); self.memset("dve", s["Sr"], 0.0, [r["Sr"]])
        for nm in ("Ss", "gtail", "btail", "ftail"):
            self.memset("dve", s[nm][tuple(slice(None) for _ in s[nm].shape)], 0.0, [r[nm]])
        npb = len(c.blocks) - 1
        for bi, (t0, TB, nsq, sl) in enumerate(c.blocks):
            if self.upto is not None and self.stage >= self.upto:
                return
            self.stage += 1
            self.pidx, self.player, self.cache_fill = 0, l, (bi == 0)
            self.ld(s["rot"][:, :, :TB], d["c_rot"][:, :, t0:t0 + TB], [r["rot"]])
            self.norm(0, t0, TB)
            self.dump(f"uT_{l}_{bi}", s["uT"][:, :, :TB], r["uT"], [128, c.KC, TB])
            if self.sub == "norm":
                continue
            self.gdn(l, bi)
            if self.sub == "gdn":
                continue
            self.dump(f"ya_{l}_{bi}", s["yT"][:, 0:c.AH, :TB], r["yT"], [128, c.AH, TB])
            self.merge(l, 0, d["w_ba"], c.AH, TB, True)
            if self.sub == "merge":
                continue
            self.ssd(l, bi)
            if self.sub == "ssd":
                continue
            self.dump(f"yb_{l}_{bi}", s["yT"][:, 0:c.BDI // 128, :TB], r["yT"], [128, c.BDI // 128, TB])
            self.merge(l, 1, d["w_bb"], c.BDI // 128, TB, False)
            self.ret(l, bi)
            if self.sub == "ret":
                continue
            self.dump(f"yc_{l}_{bi}", s["yT"][:, 0:c.CH * 2, :TB], r["yT"], [128, c.CH * 2, TB])
            self.merge(l, 2, d["w_bc"], c.CH * 2, TB, False)
            self.dump(f"hT_{l}_{bi}", s["hT"][:, :, :TB], r["hT"], [128, c.KC, TB])
            self.out_proj(l, t0, TB)
            if self.sub == "out":
                continue
            self.norm(1, t0, TB)
            self.P.barrier()
            self.ffn(l, bi)
            self.flush_panel_store()
            self.P.barrier()
            if l == c.L - 1:
                self.norm(2, t0, TB, final=True)
            if bi == npb - 1:
                o = self.out_res
                self.st(d["o_gdn_p"][l].rearrange("h k v -> k h v"), s["Sg"][:, :, :], [r["Sg"]], sem_res=o)
                self.st(d["o_ssm_p"][l].rearrange("h n p -> n h p"), s["Ss"][:, :, :], [r["Ss"]], sem_res=o)
                for kc in range(2):
                    self.st(d["o_ret_p"][l][:, kc * 128:(kc + 1) * 128, :].rearrange("h p v -> p h v"), s["Sr"][:, :, kc, :], [r["Sr"]], sem_res=o)
                self.st(d["o_gcv_p"][l], s["gtail"][:, :, :], [r["gtail"]], sem_res=o)
                self.st(d["o_scv_p"][l], s["btail"][:, :, :], [r["btail"]], sem_res=o)
                self.st(d["o_fcv_p"][l], s["ftail"][:, :, :], [r["ftail"]], sem_res=o)

    def build(self):
        self.setup()
        self.alloc_mixer()
        for l in range(self.c.L):
            self.layer(l)
        self.P.finish(final_res=[self.out_res] + self.r_xw)
        return self.nc


def make_consts(cfg, core):
    c = cfg
    out = {}
    out["c_ident"] = np.eye(128, dtype=np.float32)
    out["c_ones"] = np.ones((128, 128), np.float32)
    BIG = 30000.0
    m = np.zeros((64, 2, 4, 64), np.float32)
    ii = np.arange(64)
    rdt = np.zeros((64, 2, c.CH, 64), np.float32)
    rqd = np.zeros((128, 2, c.CH, 64), np.float32)
    rkd = np.zeros((64, 2, c.CH, 8), np.float32)
    gam = (1.0 - 2.0 ** (-5.0 - np.arange(c.CH))).astype(np.float64)
    for kind, csl in ((0, 64), (1, 8)):
        seq = ii // csl
        same = seq[:, None] == seq[None, :]
        a, b = ii[:, None], ii[None, :]
        m[:, kind, 0, :] = (same & (a <= b))
        m[:, kind, 1, :] = (a == (seq[None, :] + 1) * csl - 1)
        m[:, kind, 2, :] = np.where(same & (b < a), 0.0, -BIG)
        m[:, kind, 3, :] = np.where(same & (b >= a), 0.0, -BIG)
        pos = ii % csl
        for h in range(c.CH):
            rdt[:, kind, h, :] = np.where(same & (b >= a), gam[h] ** np.maximum(b - a, 0), 0.0)
            rqd[:, kind, h, :] = (gam[h] ** (pos + 1))[None, :]
            for sq in range(64 // csl if kind == 1 else 1):
                rkd[:, kind, h, sq] = np.where(seq == sq, gam[h] ** (csl - 1 - pos), 0.0)
    out["c_m64"] = m
    out["c_rdt"], out["c_rqd"], out["c_rkd"] = rdt, rqd, rkd
    colm = np.zeros((128, 8, 64), np.float32)
    rowm = np.zeros((64, 8), np.float32)
    for sq in range(8):
        colm[:, sq, sq * 8:(sq + 1) * 8] = 1.0
        rowm[sq * 8:(sq + 1) * 8, sq] = 1.0
    out["c_colm"], out["c_rowm"] = colm, rowm
    half = 128
    inv = (10000.0 ** (-np.arange(half, dtype=np.float32) / half)).astype(np.float32)
    pos = np.concatenate([np.arange(c.TP), np.tile(c.past + np.arange(8), c.NS)]).astype(np.float32)
    ang = (pos[None, :] * inv[:, None]).astype(np.float32)
    out["c_rot"] = np.stack([np.cos(ang), np.sin(ang)], 1).astype(np.float32)
    return out


def fmaj(v, L):
    return np.ascontiguousarray(v.reshape(L, -1, 128).transpose(0, 2, 1))


def prep_shared(cfg, p):
    c, L = cfg, cfg.L
    sh = {}
    sh["g_mix"], sh["g_ffn"] = fmaj(p["norm_mix"], L), fmaj(p["norm_ffn"], L)
    sh["g_fin"] = fmaj(p["norm_final"][None], 1)[0]
    sh["w_in"], sh["w_ba"], sh["w_bb"], sh["w_bc"] = p["w_in"], p["w_branch_a"], p["w_branch_b"], p["w_branch_c"]
    sh["w_out"], sh["w_fg"], sh["w_fu"], sh["w_fd"] = p["w_out"], p["w_ffn_gate"], p["w_ffn_up"], p["w_ffn_down"]

    def cw(w):
        Wd = w.shape[1]
        return np.ascontiguousarray(w.transpose(0, 2, 1).reshape(L, -1, 128, Wd).transpose(0, 2, 1, 3))
    sh["a_cw"], sh["b_cw"], sh["f_cw"] = cw(p["gdn_conv_w"]), cw(p["ssm_conv_w"]), cw(p["ffn_conv_w"])
    sh["b_cb"], sh["f_cb"] = fmaj(p["ssm_conv_b"], L), fmaj(p["ffn_conv_b"], L)
    sh["a_alog"], sh["a_dtb"] = p["gdn_a_log"][:, None, :], p["gdn_dt_bias"][:, None, :]
    sh["a_norm"] = p["gdn_norm"][:, :, None]
    sh["b_alog"], sh["b_dtb"], sh["b_d"] = p["ssm_a_log"][:, None, :], p["ssm_dt_bias"][:, None, :], p["ssm_d"][:, None, :]
    sh["b_norm"] = fmaj(p["ssm_norm"], L)
    return {k: np.ascontiguousarray(v, dtype=np.float32) for k, v in sh.items()}


def cst_in(v, L):
    _, NS, W, C = v.shape
    return np.ascontiguousarray(v.transpose(0, 3, 1, 2).reshape(L, C // 128, 128, NS, W).transpose(0, 2, 1, 3, 4))


def cst_out_s(v):
    L, _, NCk, NS, W = v.shape
    return np.ascontiguousarray(v.transpose(0, 2, 1, 3, 4).reshape(L, NCk * 128, NS, W).transpose(0, 2, 3, 1))


def cst_out_p(v):
    L, _, NCk, W = v.shape
    return np.ascontiguousarray(v.transpose(0, 2, 1, 3).reshape(L, NCk * 128, W).transpose(0, 2, 1))


def core_inputs(cfg, p, shared, core, nprompt):
    c = cfg
    ps = core % nprompt
    sl_ = slice(core * c.NS, (core + 1) * c.NS)
    m = dict(shared)
    x = np.concatenate([p["x_prompt"][ps], p["x_sample"][sl_].reshape(c.TS, c.D)], 0)
    m["xT"] = np.ascontiguousarray(x.T.reshape(c.KC, 128, c.TT))
    m["s_gdn"] = np.ascontiguousarray(p["state_gdn"][:, sl_])
    m["s_ssm"] = np.ascontiguousarray(p["state_ssm"][:, sl_])
    m["s_ret"] = np.ascontiguousarray(p["state_ret"][:, sl_])
    m["s_gcv"] = cst_in(p["state_gdn_conv"][:, sl_], c.L)
    m["s_scv"] = cst_in(p["state_ssm_conv"][:, sl_], c.L)
    m["s_fcv"] = cst_in(p["state_ffn_conv"][:, sl_], c.L)
    m.update(make_consts(c, core))
    return m


def assemble(cfg, res, ncores, nprompt):
    c = cfg
    D = c.D
    y_p = np.zeros((nprompt, c.TP, D), np.float32)
    y_s = np.zeros((ncores * c.NS, 8, D), np.float32)
    names_p = ["gdn_p", "gcv_p", "ssm_p", "scv_p", "ret_p", "fcv_p"]
    outp = {n: [None] * nprompt for n in names_p}
    outs = {n: [None] * ncores for n in ["gdn_s", "gcv_s", "ssm_s", "scv_s", "ret_s", "fcv_s"]}
    for core in range(ncores):
        rr = res[core]
        y = rr["yT"].reshape(D, c.TT).T
        if core < nprompt:
            y_p[core] = y[:c.TP]
            outp["gdn_p"][core] = rr["o_gdn_p"]; outp["ssm_p"][core] = rr["o_ssm_p"]; outp["ret_p"][core] = rr["o_ret_p"]
            outp["gcv_p"][core] = cst_out_p(rr["o_gcv_p"]); outp["scv_p"][core] = cst_out_p(rr["o_scv_p"])
            outp["fcv_p"][core] = cst_out_p(rr["o_fcv_p"])
        y_s[core * c.NS:(core + 1) * c.NS] = y[c.TP:].reshape(c.NS, 8, D)
        outs["gdn_s"][core] = rr["o_gdn_s"]; outs["ssm_s"][core] = rr["o_ssm_s"]; outs["ret_s"][core] = rr["o_ret_s"]
        outs["gcv_s"][core] = cst_out_s(rr["o_gcv_s"]); outs["scv_s"][core] = cst_out_s(rr["o_scv_s"])
        outs["fcv_s"][core] = cst_out_s(rr["o_fcv_s"])
    P_ = {n: np.ascontiguousarray(np.stack(v, 1)) for n, v in outp.items()}
    S_ = {n: np.ascontiguousarray(np.concatenate(v, 1)) for n, v in outs.items()}
    return (y_p, y_s, P_["gdn_p"], P_["gcv_p"], P_["ssm_p"], P_["scv_p"], P_["ret_p"], P_["fcv_p"],
            S_["gdn_s"], S_["gcv_s"], S_["ssm_s"], S_["scv_s"], S_["ret_s"], S_["fcv_s"])


def run(cfg, p, ncores, nprompt, dbg=(), upto=None, sub=None):
    kb = K(cfg, dbg=dbg)
    kb.upto, kb.sub = upto, sub
    nc = kb.build()
    shared = prep_shared(cfg, p)
    in_maps = [core_inputs(cfg, p, shared, core, nprompt) for core in range(ncores)]
    res = run_bass_kernel_spmd(nc, in_maps, core_ids=list(range(ncores)))
    return assemble(cfg, res.results, ncores, nprompt), res.results, kb


def kernel(**inputs):
    p = {k: np.asarray(v) for k, v in inputs.items()}
    out, _, _ = run(REAL, p, 8, 4)
    return out
```
